# Optimizing a Trainium2 kernel written in Bass

```python
import math
import jax, jax.numpy as jnp
from jax import lax
import numpy as np

D_MODEL = 1024
BATCH = 4
SEQ = 4096
DEPTH = 1

SSD_EXPAND = 2
D_INNER = SSD_EXPAND * D_MODEL
SSD_HEAD_DIM = 64
SSD_HEADS = D_INNER // SSD_HEAD_DIM
SSD_GROUPS = 8
D_STATE = 128
CONV_K = 4
CONV_DIM = D_INNER + 2 * SSD_GROUPS * D_STATE
SSD_CHUNK = 128
SB_HEADS = 16
SB_HEAD_DIM = 64
SB_WIDTH = SB_HEADS * SB_HEAD_DIM
SB_BLOCK = 128
D_FF = 4 * D_MODEL
PLE_DIM = 256
N_BRANCHES = 2
RMS_EPS = 1e-6
IN_PROJ_DIM = D_INNER + CONV_DIM + SSD_HEADS + 3 * SB_WIDTH
SPLITS = [D_INNER, D_INNER + CONV_DIM, D_INNER + CONV_DIM + SSD_HEADS,
          D_INNER + CONV_DIM + SSD_HEADS + SB_WIDTH,
          D_INNER + CONV_DIM + SSD_HEADS + 2 * SB_WIDTH]

kernel_name = "hybrid_ssd_stickbreak_gated_block"


def rms_norm(x, w):
    xf = x.astype(jnp.float32)
    y = xf * lax.rsqrt(jnp.mean(xf * xf, axis=-1, keepdims=True) + RMS_EPS)
    return (y * w.astype(jnp.float32)).astype(x.dtype)


def causal_depthwise_conv(u, w, b):
    k = w.shape[0]
    c = u.shape[-1]
    y = lax.conv_general_dilated(u, w[:, None, :].astype(u.dtype), window_strides=(1,),
                                 padding=[(k - 1, 0)],
                                 dimension_numbers=("NWC", "WIO", "NWC"),
                                 feature_group_count=c)
    return y + b.astype(u.dtype)


def ssd_chunked_scan(xs, dt, a, bmat, cmat):
    bsz, seqlen, nh, hd = xs.shape
    ng, ns = bmat.shape[2], bmat.shape[3]
    r = nh // ng
    nc = seqlen // SSD_CHUNK
    shp = (bsz, nc, SSD_CHUNK, ng, r)
    xd = (xs * dt[..., None]).reshape(shp + (hd,))
    a_cs = jnp.cumsum((dt * a).reshape(shp), axis=2)
    bc = bmat.reshape(bsz, nc, SSD_CHUNK, ng, ns)
    cc = cmat.reshape(bsz, nc, SSD_CHUNK, ng, ns)
    seg = a_cs[:, :, :, None] - a_cs[:, :, None, :]
    causal = jnp.tril(jnp.ones((SSD_CHUNK, SSD_CHUNK), dtype=bool))[None, None, :, :, None, None]
    decay = jnp.exp(jnp.where(causal, seg, -jnp.inf))
    cb = jnp.einsum("bclgn,bcsgn->bclsg", cc, bc)
    y_diag = jnp.einsum("bclsgr,bcsgrp->bclgrp", cb[..., None] * decay, xd)
    decay_to_end = jnp.exp(a_cs[:, :, -1:] - a_cs)
    states = jnp.einsum("bcsgn,bcsgrp->bcgrpn", bc, xd * decay_to_end[..., None])
    chunk_decay = jnp.exp(a_cs[:, :, -1])

    def step(carry, inp):
        st, dec = inp
        return carry * dec[..., None, None] + st, carry

    init = jnp.zeros((bsz, ng, r, hd, ns), jnp.float32)
    _, prev = lax.scan(step, init, (jnp.moveaxis(states, 1, 0), jnp.moveaxis(chunk_decay, 1, 0)))
    prev = jnp.moveaxis(prev, 0, 1)
    y_off = jnp.einsum("bclgn,bcgrpn->bclgrp", cc, prev) * jnp.exp(a_cs)[..., None]
    return (y_diag + y_off).reshape(bsz, seqlen, nh, hd)


def ssd_mixer(z, xbc, dt_raw, conv_w, conv_b, dt_bias, a_log, d_skip, norm_w):
    bsz, seqlen, _ = z.shape
    xbc = jax.nn.silu(causal_depthwise_conv(xbc, conv_w, conv_b))
    xs, bmat, cmat = jnp.split(xbc, [D_INNER, D_INNER + SSD_GROUPS * D_STATE], axis=-1)
    xs = xs.astype(jnp.float32).reshape(bsz, seqlen, SSD_HEADS, SSD_HEAD_DIM)
    bmat = bmat.astype(jnp.float32).reshape(bsz, seqlen, SSD_GROUPS, D_STATE)
    cmat = cmat.astype(jnp.float32).reshape(bsz, seqlen, SSD_GROUPS, D_STATE)
    dt = jax.nn.softplus(dt_raw.astype(jnp.float32) + dt_bias.astype(jnp.float32))
    a = -jnp.exp(a_log.astype(jnp.float32))
    y = ssd_chunked_scan(xs, dt, a, bmat, cmat) + xs * d_skip.astype(jnp.float32)[:, None]
    y = y.reshape(bsz, seqlen, D_INNER) * jax.nn.silu(z.astype(jnp.float32))
    yg = y.reshape(bsz, seqlen, SSD_GROUPS, D_INNER // SSD_GROUPS)
    yg = yg * lax.rsqrt(jnp.mean(yg * yg, axis=-1, keepdims=True) + RMS_EPS)
    y = yg.reshape(bsz, seqlen, D_INNER) * norm_w.astype(jnp.float32)
    return y.astype(z.dtype)


def stick_breaking_attention(q, k, v):
    seqlen, hd = q.shape[2], q.shape[3]
    scale = hd ** -0.5
    outs = []
    for blk in range(seqlen // SB_BLOCK):
        t0 = blk * SB_BLOCK
        t1 = t0 + SB_BLOCK
        logits = jnp.einsum("bhtd,bhsd->bhts", q[:, :, t0:t1], k[:, :, :t1]).astype(jnp.float32) * scale
        mask = jnp.arange(t1)[None, :] < jnp.arange(t0, t1)[:, None]
        log_keep = jnp.where(mask, jax.nn.log_sigmoid(-logits), 0.0)
        later = jnp.flip(jnp.cumsum(jnp.flip(log_keep, -1), axis=-1), -1) - log_keep
        weights = jnp.where(mask, jnp.exp(jax.nn.log_sigmoid(logits) + later), 0.0)
        outs.append(jnp.einsum("bhts,bhsd->bhtd", weights, v[:, :, :t1].astype(jnp.float32)))
    return jnp.concatenate(outs, axis=2)


def setup_inputs(seed: int = 0) -> dict:
    key = jax.random.key(seed)
    ks = jax.random.split(key, 26)
    f32 = jnp.float32

    def nrm(k, shape, fan_in):
        return jax.random.normal(k, shape, f32) * (fan_in ** -0.5)

    def gain(k, shape):
        return 1.0 + 0.05 * jax.random.normal(k, shape, f32)

    dt0 = jnp.exp(jax.random.uniform(ks[6], (DEPTH, SSD_HEADS), f32)
                  * (math.log(0.1) - math.log(0.001)) + math.log(0.001))
    dt_bias = dt0 + jnp.log(-jnp.expm1(-dt0))
    a_log = jnp.log(jax.random.uniform(ks[7], (DEPTH, SSD_HEADS), f32, 1.0, 16.0))
    return {
        "x": jax.random.normal(ks[0], (BATCH, SEQ, D_MODEL), f32),
        "p": jax.random.normal(ks[1], (DEPTH, BATCH, SEQ, PLE_DIM), f32),
        "norm_mix_pre": gain(ks[2], (DEPTH, D_MODEL)),
        "w_in": nrm(ks[3], (DEPTH, D_MODEL, IN_PROJ_DIM), D_MODEL),
        "conv_w": nrm(ks[4], (DEPTH, CONV_K, CONV_DIM), CONV_K),
        "conv_b": 0.01 * jax.random.normal(ks[5], (DEPTH, CONV_DIM), f32),
        "dt_bias": dt_bias,
        "a_log": a_log,
        "d_skip": 1.0 + 0.1 * jax.random.normal(ks[8], (DEPTH, SSD_HEADS), f32),
        "ssd_norm": gain(ks[9], (DEPTH, D_INNER)),
        "w_ssd_branch": nrm(ks[10], (DEPTH, D_INNER, D_MODEL), D_INNER),
        "w_sb_branch": nrm(ks[11], (DEPTH, SB_WIDTH, D_MODEL), SB_WIDTH),
        "w_gate": nrm(ks[12], (DEPTH, D_MODEL, N_BRANCHES * D_MODEL), D_MODEL),
        "b_gate": 0.01 * jax.random.normal(ks[13], (DEPTH, N_BRANCHES * D_MODEL), f32),
        "w_out": nrm(ks[14], (DEPTH, D_MODEL, D_MODEL), D_MODEL),
        "norm_mix_post": gain(ks[15], (DEPTH, D_MODEL)),
        "norm_ffn_pre": gain(ks[16], (DEPTH, D_MODEL)),
        "w_ff1": nrm(ks[17], (DEPTH, D_MODEL, D_FF), D_MODEL),
        "w_ff2": nrm(ks[18], (DEPTH, D_FF, D_MODEL), D_FF),
        "norm_ffn_post": gain(ks[19], (DEPTH, D_MODEL)),
        "w_ple": nrm(ks[20], (DEPTH, PLE_DIM, D_MODEL), PLE_DIM),
        "w_ple_gate": nrm(ks[21], (DEPTH, D_MODEL, D_MODEL), D_MODEL),
        "norm_ple_post": gain(ks[22], (DEPTH, D_MODEL)),
    }


def reference(x, p, norm_mix_pre, w_in, conv_w, conv_b, dt_bias, a_log, d_skip, ssd_norm,
              w_ssd_branch, w_sb_branch, w_gate, b_gate, w_out, norm_mix_post,
              norm_ffn_pre, w_ff1, w_ff2, norm_ffn_post, w_ple, w_ple_gate, norm_ple_post):
    h = x
    bsz, seqlen, _ = x.shape
    for i in range(DEPTH):
        n1 = rms_norm(h, norm_mix_pre[i])
        proj = n1 @ w_in[i]
        z, xbc, dt_raw, q, k, v = jnp.split(proj, SPLITS, axis=-1)
        y_ssd = ssd_mixer(z, xbc, dt_raw, conv_w[i], conv_b[i], dt_bias[i], a_log[i],
                          d_skip[i], ssd_norm[i])
        heads = lambda t: t.reshape(bsz, seqlen, SB_HEADS, SB_HEAD_DIM).transpose(0, 2, 1, 3)
        y_sb = stick_breaking_attention(heads(q), heads(k), heads(v))
        y_sb = y_sb.transpose(0, 2, 1, 3).reshape(bsz, seqlen, SB_WIDTH).astype(h.dtype)
        gates = jax.nn.sigmoid((n1 @ w_gate[i] + b_gate[i]).astype(jnp.float32)).astype(h.dtype)
        g_ssd, g_sb = jnp.split(gates, N_BRANCHES, axis=-1)
        merged = g_ssd * (y_ssd @ w_ssd_branch[i]) + g_sb * (y_sb @ w_sb_branch[i])
        h = h + rms_norm(merged @ w_out[i], norm_mix_post[i])
        n2 = rms_norm(h, norm_ffn_pre[i])
        ff = jnp.square(jax.nn.relu(n2 @ w_ff1[i])) @ w_ff2[i]
        h = h + rms_norm(ff, norm_ffn_post[i])
        ple_gate = jax.nn.sigmoid((h @ w_ple_gate[i]).astype(jnp.float32)).astype(h.dtype)
        h = h + rms_norm(ple_gate * (p[i].astype(h.dtype) @ w_ple[i]), norm_ple_post[i])
    return h
```

```python
import contextlib
import numpy as np
import concourse.bass as bass
import concourse.mybir as mybir
from concourse.bass_utils import run_bass_kernel_spmd

F32 = mybir.dt.float32
BF16 = mybir.dt.bfloat16
AF = mybir.ActivationFunctionType
ALU = mybir.AluOpType

NPOS = 4096
NBLK = 32
NOWN = 2048
NOB = 16
EPS = 1e-6
OFF_Z, OFF_X, OFF_B, OFF_C, OFF_DT, OFF_Q, OFF_K, OFF_V = 0, 2048, 4096, 5120, 6144, 6176, 7200, 8224
NEG = -30000.0


ALL_T = []


class T:
    __slots__ = ("h", "hb", "w", "r", "sem", "cnt", "name", "bg")

    def __init__(self, h, name="", hb=None):
        self.h = h
        self.hb = hb
        self.w = {}
        self.r = {}
        self.sem = None
        self.cnt = 0
        self.name = name
        self.bg = False
        ALL_T.append(self)

    def __getitem__(self, k):
        return self.h[k]


class FW:
    def __init__(self, nc):
        self.nc = nc
        self.engs = {"pe": nc.tensor, "dve": nc.vector, "act": nc.scalar,
                     "pool": nc.gpsimd, "sp": nc.sync}
        self.sem = {k: nc.alloc_semaphore(name="sem_" + k) for k in self.engs}
        self.cnt = {k: 0 for k in self.engs}
        self.waited = {k: {} for k in self.engs}
        self.semobj = {id(s): s for s in self.sem.values()}
        self.dma_T = []
        self.free_sems = []
        self.ninst = 0
        self.nwaits = 0
        self.uid = 0

    def _name(self, p):
        self.uid += 1
        return f"{p}{self.uid}"

    def sb(self, stack, shape, dt, name=None):
        name = self._name(name or "sb")
        h = stack.enter_context(self.nc.sbuf_tensor(name, list(shape), dt))
        return T(h, name)

    def view(self, t):
        return T(t.h, self._name(t.name + "_v"))

    def dram(self, name, shape, dt, kind="Internal"):
        return T(self.nc.dram_tensor(name, list(shape), dt, kind=kind), name)

    def _wait(self, e, deps):
        eng = self.engs[e]
        wd = self.waited[e]
        own = id(self.sem[e])
        for sid, val in deps.items():
            if val <= 0 or wd.get(sid, 0) >= val:
                continue
            if e == "pe" and sid == own:
                continue
            eng.wait_ge(self.semobj[sid], val)
            wd[sid] = val
            self.nwaits += 1

    @staticmethod
    def _merge(d, s):
        for k, v in s.items():
            if d.get(k, 0) < v:
                d[k] = v

    def op(self, e, fn, reads=(), writes=()):
        deps = {}
        for t in reads:
            self._merge(deps, t.w)
        for t in writes:
            self._merge(deps, t.w)
            self._merge(deps, t.r)
        self._wait(e, deps)
        ins = fn()
        self.cnt[e] += 1
        self.ninst += 1
        ins.then_inc(self.sem[e], 1)
        d = {id(self.sem[e]): self.cnt[e]}
        for t in reads:
            self._merge(t.r, d)
        for t in writes:
            self._merge(t.w, d)

    def dma(self, q, out_ap, in_ap, dst, src, own, **kw):
        deps = {}
        self._merge(deps, src.w)
        self._merge(deps, dst.w)
        self._merge(deps, dst.r)
        self._wait(q, deps)
        if own.sem is None:
            if self.free_sems:
                own.sem, own.cnt = self.free_sems.pop()
            else:
                own.sem = self.nc.alloc_semaphore(name=f"ds{len(self.semobj)}")
                own.cnt = 0
            self.semobj[id(own.sem)] = own.sem
            self.dma_T.append(own)
        ins = self.engs[q].dma_start(out=out_ap, in_=in_ap, **kw)
        own.cnt += 16
        self.ninst += 1
        ins.then_inc(own.sem, 16)
        d = {id(own.sem): own.cnt}
        self._merge(src.r, d)
        self._merge(dst.w, d)

    def barrier(self, recycle=True):
        import os
        if os.environ.get("NORECYCLE"):
            recycle = False
        deps = {id(self.sem[k]): self.cnt[k] for k in self.engs if k != "sp"}
        fg = [t for t in self.dma_T if not t.bg]
        for t in fg:
            deps[id(t.sem)] = t.cnt
        self._wait("sp", deps)
        dead = set()
        if recycle:
            for t in fg:
                dead.add(id(t.sem))
        self.engs["sp"].sem_inc(self.sem["sp"], 1)
        self.cnt["sp"] += 1
        d = {id(self.sem["sp"]): self.cnt["sp"]}
        for k in self.engs:
            if k != "sp":
                self._wait(k, d)
        if recycle:
            for t in ALL_T:
                for k in dead:
                    t.w.pop(k, None)
                    t.r.pop(k, None)
            for k in self.engs:
                for sid in dead:
                    self.waited[k].pop(sid, None)
            for t in fg:
                self.free_sems.append((t.sem, t.cnt))
                self.dma_T.remove(t)
                t.sem = None
                t.cnt = 0


def bc_last(ap, n):
    sh = list(ap.shape)
    return ap.unsqueeze(len(sh)).to_broadcast(sh + [n])


def build(debug=0):
    del ALL_T[:]
    nc = bass.Bass("TRN2", target_bir_lowering=False)
    fw = FW(nc)
    op, dma = fw.op, fw.dma
    V, S, P, G, PE = nc.vector, nc.scalar, nc.gpsimd, nc.gpsimd, nc.tensor
    dk = "ExternalOutput" if debug else "Internal"

    def din(name, shape):
        return fw.dram(name, shape, F32, kind="ExternalInput")

    xs_d = din("xs", [NPOS, 1024])
    p_d = din("p_own", [NOWN, 256])
    valid_d = din("valid", [128, NBLK])
    nmpre_d = din("norm_mix_pre", [1024])
    win_d = din("w_in", [1024, 9248])
    convw_d = din("conv_w", [4, 4096])
    convb_d = din("conv_b", [4096])
    dtb_d = din("dt_bias", [32])
    alog_d = din("a_log", [32])
    dskip_d = din("d_skip", [32])
    ssdn_d = din("ssd_norm", [2048])
    wssd_d = din("w_ssd_branch", [2048, 1024])
    wsb_d = din("w_sb_branch", [1024, 1024])
    wgate_d = din("w_gate", [1024, 2048])
    bgate_d = din("b_gate", [2048])
    wout_d = din("w_out", [1024, 1024])
    nmpost_d = din("norm_mix_post", [1024])
    nfpre_d = din("norm_ffn_pre", [1024])
    wff1_d = din("w_ff1", [1024, 4096])
    wff2_d = din("w_ff2", [4096, 1024])
    nfpost_d = din("norm_ffn_post", [1024])
    wple_d = din("w_ple", [256, 1024])
    wpleg_d = din("w_ple_gate", [1024, 1024])
    nppost_d = din("norm_ple_post", [1024])
    out_d = fw.dram("out", [NOWN, 1024], F32, kind="ExternalOutput")

    n1To_d = fw.dram("n1To", [1024, NOWN], BF16, kind=dk)
    kT_d = fw.dram("kT", [1024, NPOS], BF16, kind=dk)
    qT_d = fw.dram("qT", [1024, NOWN], BF16, kind=dk)
    v_d = fw.dram("vtm", [NPOS, 1024], BF16, kind=dk)
    xstm_d = fw.dram("xstm", [NPOS, 2048], BF16, kind=dk)
    btm_d = fw.dram("btm", [NPOS, 1024], BF16, kind=dk)
    bT_d = fw.dram("bT", [1024, NPOS], BF16, kind=dk)
    cT_d = fw.dram("cT", [1024, NPOS], BF16, kind=dk)
    zs_d = fw.dram("zs", [NOWN, 2048], F32, kind=dk)
    dtt_d = fw.dram("dtt", [128, NBLK, 32], F32, kind=dk)
    yssdT_d = fw.dram("yssdT", [2048, NOWN], BF16, kind=dk)
    ysbT_d = fw.dram("ysbT", [1024, NOWN], BF16, kind=dk)
    h1_d = fw.dram("h1", [NOWN, 1024], F32, kind=dk)
    n2T_d = fw.dram("n2T", [1024, NOWN], BF16, kind=dk)
    h2_d = fw.dram("h2", [NOWN, 1024], F32, kind=dk)
    h2T_d = fw.dram("h2T", [1024, NOWN], BF16, kind=dk)
    wg_b = fw.dram("wg_b", [1024, 2048], BF16)
    wssd_b = fw.dram("wssd_b", [2048, 1024], BF16)
    wsb_b = fw.dram("wsb_b", [1024, 1024], BF16)
    wout_b = fw.dram("wout_b", [1024, 1024], BF16)
    w1_b = fw.dram("w1_b", [1024, 4096], BF16)
    w2_b = fw.dram("w2_b", [4096, 1024], BF16)
    wpg_b = fw.dram("wpg_b", [1024, 1024], BF16)
    wp_b = fw.dram("wp_b", [256, 1024], BF16)

    banks = []
    ps_all = nc.alloc_psum_tensor("ps_all", [128, 8 * 512], F32)
    for i in range(8):
        h = ps_all[:, i * 512:(i + 1) * 512]
        banks.append(T(h, f"bank{i}", hb=h.bitcast(BF16)))
    bank_i = [0]

    def bank():
        b = banks[bank_i[0] % 8]
        bank_i[0] += 1
        return b

    with contextlib.ExitStack() as gs:
        ident = fw.sb(gs, [128, 128], BF16, "ident")
        tmpf = fw.sb(gs, [128, 128], F32, "tmpf")
        tri = fw.sb(gs, [128, 128], F32, "tri")
        onesf = fw.sb(gs, [128, 128], F32, "onesf")
        negm = fw.sb(gs, [128, 512], BF16, "negm")
        negm_ssd = fw.sb(gs, [128, 128], BF16, "negms")
        negU = fw.sb(gs, [128, 128], BF16, "negU")
        negO = fw.sb(gs, [128, 128], BF16, "negO")
        dtt = fw.sb(gs, [128, NBLK, 32], F32, "dtt")
        valid = fw.sb(gs, [128, NBLK], F32, "valid")

        def cmask(dst, val_keep, cmp, fill, base, cm, pat, n):
            op("pool", lambda: G.memset(tmpf[:, 0:n], val_keep), writes=[tmpf])
            op("pool", lambda: G.affine_select(out=tmpf[:, 0:n], in_=tmpf[:, 0:n], pattern=[[pat, n]],
                                                compare_op=cmp, fill=fill, base=base, channel_multiplier=cm),
               reads=[tmpf], writes=[tmpf])

        cmask(None, 1.0, ALU.is_equal, 0.0, 0, 1, -1, 128)
        op("dve", lambda: V.tensor_copy(out=ident[:], in_=tmpf[:]), reads=[tmpf], writes=[ident])
        cmask(None, 1.0, ALU.is_ge, 0.0, 0, -1, 1, 128)
        op("dve", lambda: V.tensor_copy(out=tri[:], in_=tmpf[:]), reads=[tmpf], writes=[tri])
        cmask(None, 0.0, ALU.is_ge, NEG, 0, -1, 1, 128)
        op("dve", lambda: V.tensor_copy(out=negm_ssd[:], in_=tmpf[:]), reads=[tmpf], writes=[negm_ssd])
        cmask(None, 0.0, ALU.is_gt, NEG, 0, -1, 1, 128)
        for j in range(4):
            op("dve", lambda: V.tensor_copy(out=negm[:, j * 128:(j + 1) * 128], in_=tmpf[:]), reads=[tmpf], writes=[negm])
        cmask(None, -1.0, ALU.is_ge, 0.0, 0, 1, -1, 128)
        op("dve", lambda: V.tensor_copy(out=negU[:], in_=tmpf[:]), reads=[tmpf], writes=[negU])
        op("pool", lambda: G.memset(negO[:], -1.0), writes=[negO])
        op("pool", lambda: G.memset(onesf[:], 1.0), writes=[onesf])
        dma("sp", valid[:], valid_d[:, :], valid, valid_d, valid)
        identf = fw.sb(gs, [128, 128], F32, "identf")
        cmask(None, 1.0, ALU.is_equal, 0.0, 0, 1, -1, 128)
        op("dve", lambda: V.tensor_copy(out=identf[:], in_=tmpf[:]), reads=[tmpf], writes=[identf])

        def load_fm(stk, dst, c0, src_ap, n, src_t):
            rows = fw.sb(stk, [n, 128], F32, "rows")
            dma("sp", rows[:], src_ap, rows, src_t, rows)
            bk = bank()
            op("pe", lambda: PE.transpose(out=bk[:, 0:n], in_=rows[:], identity=identf[0:n, 0:n]), reads=[rows, identf], writes=[bk])
            op("dve", lambda: V.tensor_copy(out=dst[:, c0:c0 + n], in_=bk[:, 0:n]), reads=[bk], writes=[dst])

        with contextlib.ExitStack() as st:
            n1T = fw.sb(st, [128, 8, NPOS], BF16, "n1T")
            n1c = [fw.view(n1T) for _ in range(8)]
            gfm = fw.sb(st, [128, 8], F32, "gfm")
            load_fm(st, gfm, 0, nmpre_d.h.rearrange("(k p) -> k p", p=128), 8, nmpre_d)
            wblk = [fw.sb(st, [128, 8, 512], BF16, "wblk") for _ in range(3)]
            stg = [fw.sb(st, [128, NPOS], BF16, "stg") for _ in range(3)]
            stA = contextlib.ExitStack()
            if True:
                xt = [fw.sb(stA, [128, 1024], F32, "xt") for _ in range(3)]
                junk = fw.sb(stA, [128, 1024], F32, "junk")
                ssq = [fw.sb(stA, [128, 1], F32, "ssq") for _ in range(3)]
                xn = [fw.sb(stA, [128, 1024], BF16, "xn") for _ in range(3)]
                pbanks = {}

                def P1(pb):
                    X, SS, XN = xt[pb % 3], ssq[pb % 3], xn[pb % 3]
                    dma("sp", X[:], xs_d[pb * 128:(pb + 1) * 128, :], X, xs_d, X)
                    op("dve", lambda: V.scalar_tensor_tensor(out=junk[:], in0=X[:], scalar=1.0, in1=X[:], op0=ALU.mult,
                                                             op1=ALU.mult, accum_out=SS[:]), reads=[X], writes=[SS])
                    op("act", lambda: S.activation(out=SS[:], in_=SS[:], func=AF.Sqrt, scale=1.0 / 1024, bias=EPS), reads=[SS], writes=[SS])
                    op("dve", lambda: V.reciprocal(out=SS[:], in_=SS[:]), reads=[SS], writes=[SS])
                    op("act", lambda: S.activation(out=XN[:], in_=X[:], func=AF.Identity, scale=SS[:]), reads=[X, SS], writes=[XN])
                    bk = bank()
                    pbanks[pb] = bk
                    pT = bk.hb.rearrange("p (k c) -> p k c", k=8)
                    for k in range(8):
                        op("pe", lambda: PE.transpose(out=pT[:, k, :], in_=XN[:, k * 128:(k + 1) * 128], identity=ident[:]), reads=[XN, ident], writes=[bk])

                def P2(pb):
                    bk = pbanks.pop(pb)
                    pT = bk.hb.rearrange("p (k c) -> p k c", k=8)
                    op("dve", lambda: V.tensor_tensor(out=n1T[:, :, pb * 128:(pb + 1) * 128], in0=pT, in1=bc_last(gfm[:], 128), op=ALU.mult),
                       reads=[bk, gfm], writes=[n1c[pb // 4]])

                P1(0)
                for pb in range(NBLK):
                    if pb + 1 < NBLK:
                        P1(pb + 1)
                    P2(pb)
            n1own = n1T.h.rearrange("p k (i two j) -> p k i two j", two=2, j=128)
            for c in n1c:
                fw._merge(n1T.w, c.w)
            for k in range(8):
                dma("sp", n1To_d.h.rearrange("(k p) (i j) -> p k i j", p=128, j=128)[:, k, :, :], n1own[:, k, :, 1, :], n1To_d, n1T, n1T)
            for c in n1c:
                fw._merge(c.r, n1T.r)

            wi = [0]
            win_v = win_d.h.rearrange("(k p) c -> p k c", p=128)

            def loadw(c0, n=512):
                w = wblk[wi[0] % 3]
                wi[0] += 1
                dma("pool", w[:, :, 0:n], win_v[:, :, c0:c0 + n], w, win_d, w)
                return w

            si = [0]

            def mm_fm(w, ft, rhs_of, nchunk, evac):
                for pc in range(nchunk):
                    bk = bank()
                    rd = rhs_of(pc)
                    for kc in range(8):
                        op("pe", lambda: PE.matmul(bk[:, :], lhsT=w[:, kc, ft * 128:(ft + 1) * 128], rhs=rd[0](kc),
                                                   start=(kc == 0), stop=(kc == 7)), reads=[w, rd[1]], writes=[bk])
                    evac(pc, bk)

            def rhs_all(pc):
                return (lambda kc: n1T[:, kc, pc * 512:(pc + 1) * 512], n1c[pc])

            def rhs_own(pc):
                return (lambda kc: n1own[:, kc, 4 * pc:4 * pc + 4, 1, :], n1c[2 * pc])

            for cb in range(2):
                w = loadw(OFF_K + cb * 512)
                for ft in range(4):
                    sg = stg[si[0] % 3]
                    si[0] += 1
                    mm_fm(w, ft, rhs_all, 8, lambda pc, bk: op("act", lambda: S.copy(out=sg[:, pc * 512:(pc + 1) * 512], in_=bk[:, :]), reads=[bk], writes=[sg]))
                    r0 = (cb * 4 + ft) * 128
                    dma("sp", kT_d[r0:r0 + 128, :], sg[:, :], kT_d, sg, sg)
            for cb in range(2):
                w = loadw(OFF_Q + cb * 512)
                for ft in range(4):
                    sg = stg[si[0] % 3]
                    si[0] += 1
                    for pc in range(4):
                        bk = bank()
                        for kc in range(8):
                            op("pe", lambda: PE.matmul(bk[:, :], lhsT=w[:, kc, ft * 128:(ft + 1) * 128], rhs=n1own[:, kc, 4 * pc:4 * pc + 4, 1, :],
                                                       start=(kc == 0), stop=(kc == 7)), reads=[w, n1c[2 * pc], n1c[2 * pc + 1]], writes=[bk])
                        op("act", lambda: S.activation(out=sg[:, pc * 512:(pc + 1) * 512], in_=bk[:, :], func=AF.Copy, scale=0.125), reads=[bk], writes=[sg])
                    r0 = (cb * 4 + ft) * 128
                    dma("sp", qT_d[r0:r0 + 128, :], sg[:, 0:NOWN], qT_d, sg, sg)
            stB1 = contextlib.ExitStack()
            wv = [loadw(OFF_V), loadw(OFF_V + 512)]
            vst = [fw.sb(stB1, [128, 1024], BF16, "vst") for _ in range(2)]
            for pb in range(NBLK):
                vs = vst[pb % 2]
                for cb in range(2):
                    bk = bank()
                    for kc in range(8):
                        op("pe", lambda: PE.matmul(bk[:, :], lhsT=n1T[:, kc, pb * 128:(pb + 1) * 128], rhs=wv[cb][:, kc, :],
                                                   start=(kc == 0), stop=(kc == 7)), reads=[wv[cb], n1c[pb // 4]], writes=[bk])
                    op("act", lambda: S.copy(out=vs[:, cb * 512:(cb + 1) * 512], in_=bk[:, :]), reads=[bk], writes=[vs])
                dma("sp", v_d[pb * 128:(pb + 1) * 128, :], vs[:, :], v_d, vs, vs)
            wdt = loadw(OFF_DT, 32)
            dtb = fw.sb(stB1, [128, 32], F32, "dtb")
            dma("sp", dtb[:], dtb_d.h.ap().partition_broadcast(128), dtb, dtb_d, dtb)
            dtmp = fw.sb(stB1, [128, 16, 32], F32, "dtmp")
            for half in range(2):
                bk = bank()
                bkv = bk.h.rearrange("p (a b) -> p a b", b=32)
                for j in range(16):
                    pb = half * 16 + j
                    for kc in range(8):
                        op("pe", lambda: PE.matmul(bkv[:, j, :], lhsT=n1T[:, kc, pb * 128:(pb + 1) * 128], rhs=wdt[:, kc, 0:32],
                                                   start=(kc == 0), stop=(kc == 7)), reads=[wdt, n1c[pb // 4]], writes=[bk])
                op("dve", lambda: V.tensor_tensor(out=dtmp[:], in0=bkv, in1=dtb[:].unsqueeze(1).to_broadcast([128, 16, 32]), op=ALU.add),
                   reads=[bk, dtb], writes=[dtmp])
                op("act", lambda: S.activation(out=dtmp[:], in_=dtmp[:], func=AF.Exp), reads=[dtmp], writes=[dtmp])
                op("act", lambda: S.activation(out=dtmp[:], in_=dtmp[:], func=AF.Ln, bias=1.0), reads=[dtmp], writes=[dtmp])
                op("dve", lambda: V.tensor_tensor(out=dtt[:, half * 16:(half + 1) * 16, :], in0=dtmp[:],
                                                  in1=bc_last(valid[:, half * 16:(half + 1) * 16], 32), op=ALU.mult),
                   reads=[dtmp, valid], writes=[dtt])
            if debug:
                dma("sp", dtt_d[:, :, :], dtt[:], dtt_d, dtt, dtt)
            zst = [fw.sb(stB1, [128, 512], F32, "zst") for _ in range(2)]
            zi = 0
            for cb in range(4):
                w = loadw(OFF_Z + cb * 512)
                for i in range(NOB):
                    pb = 2 * i + 1
                    bk = bank()
                    for kc in range(8):
                        op("pe", lambda: PE.matmul(bk[:, :], lhsT=n1T[:, kc, pb * 128:(pb + 1) * 128], rhs=w[:, kc, :],
                                                   start=(kc == 0), stop=(kc == 7)), reads=[w, n1c[pb // 4]], writes=[bk])
                    zt = zst[zi % 2]
                    zi += 1
                    op("act", lambda: S.activation(out=zt[:], in_=bk[:, :], func=AF.Silu), reads=[bk], writes=[zt])
                    dma("sp", zs_d[i * 128:(i + 1) * 128, cb * 512:(cb + 1) * 512], zt[:], zs_d, zt, zt)
            fw.barrier()
            stB1.close()
            stA.close()
            cw = fw.sb(st, [128, 4 * 32], F32, "cw")
            cbias = fw.sb(st, [128, 32], F32, "cbias")
            for k in range(4):
                load_fm(st, cw, k * 32, convw_d.h.rearrange("k (f p) -> k f p", p=128)[k], 32, convw_d)
            load_fm(st, cbias, 0, convb_d.h.rearrange("(f p) -> f p", p=128), 32, convb_d)
            pre = [fw.sb(st, [128, NPOS + 3], F32, "pre") for _ in range(2)]
            acc = [fw.sb(st, [128, NPOS], F32, "acc") for _ in range(2)]
            tm = [fw.sb(st, [128, NBLK, 128], BF16, "tm") for _ in range(2)]
            for p_ in pre:
                op("pool", lambda: G.memset(p_[:, 0:3], 0.0), writes=[p_])
            wx = {}
            sgs = {}

            def Mx(f):
                cb, ft = f // 4, f % 4
                if ft == 0:
                    wx[cb] = loadw(OFF_X + cb * 512)
                w, PRE = wx[cb], pre[f % 2]
                mm_fm(w, ft, rhs_all, 8, lambda pc, bk: op("act", lambda: S.copy(out=PRE[:, 3 + pc * 512:3 + (pc + 1) * 512], in_=bk[:, :]), reads=[bk], writes=[PRE]))

            def C1(f):
                PRE, ACC = pre[f % 2], acc[f % 2]
                op("act", lambda: S.activation(out=ACC[:], in_=PRE[:, 0:NPOS], func=AF.Identity, scale=cw[:, f:f + 1]), reads=[PRE, cw], writes=[ACC])
                for k in range(1, 4):
                    op("dve", lambda: V.scalar_tensor_tensor(out=ACC[:], in0=PRE[:, k:k + NPOS], scalar=cw[:, k * 32 + f:k * 32 + f + 1], in1=ACC[:],
                                                             op0=ALU.mult, op1=ALU.add), reads=[PRE, cw, ACC], writes=[ACC])

            def C2(f):
                ACC = acc[f % 2]
                sg = stg[si[0] % 3]
                si[0] += 1
                sgs[f] = sg
                op("act", lambda: S.activation(out=sg[:], in_=ACC[:], func=AF.Silu, bias=cbias[:, f:f + 1]), reads=[ACC, cbias], writes=[sg])
                if f >= 16:
                    dT = bT_d if f < 24 else cT_d
                    r0 = ((f - 16) % 8) * 128
                    dma("sp", dT[r0:r0 + 128, :], sg[:, :], dT, sg, sg)

            def Tx(f):
                sg = sgs.pop(f)
                if f >= 24:
                    return
                tmt = tm[f % 2]
                for grp in range(4):
                    bk = bank()
                    pT = bk.hb.rearrange("p (k c) -> p k c", k=8)
                    for j in range(8):
                        pb = grp * 8 + j
                        op("pe", lambda: PE.transpose(out=pT[:, j, :], in_=sg[:, pb * 128:(pb + 1) * 128], identity=ident[:]), reads=[sg, ident], writes=[bk])
                    if grp % 2 == 0:
                        op("act", lambda: S.copy(out=tmt[:, grp * 8:(grp + 1) * 8, :], in_=pT), reads=[bk], writes=[tmt])
                    else:
                        op("dve", lambda: V.tensor_copy(out=tmt[:, grp * 8:(grp + 1) * 8, :], in_=pT), reads=[bk], writes=[tmt])
                dd, c0 = (xstm_d, f * 128) if f < 16 else (btm_d, (f - 16) * 128)
                dma("sp", dd.h.rearrange("(b p) c -> p b c", p=128)[:, :, c0:c0 + 128], tmt[:], dd, tmt, tmt)

            Mx(0)
            for f in range(32 + 2):
                if f + 1 < 32:
                    Mx(f + 1)
                if f < 32:
                    C1(f)
                if 0 <= f - 1 < 32:
                    C2(f - 1)
                if 0 <= f - 2 < 32:
                    Tx(f - 2)
            fw.barrier()
        print("phaseAB ninst", fw.ninst, "nwaits", fw.nwaits)
        if debug == 1:
            dma("sp", out_d[0:128, :], xs_d[0:128, :], out_d, xs_d, out_d)
            fw.barrier()
            return nc

        with contextlib.ExitStack() as st:
            a_rep = fw.sb(st, [128, 32], F32, "a_rep")
            dsk = fw.sb(st, [128, 32], F32, "dsk")
            ssdn = fw.sb(st, [128, 2048], F32, "ssdn")
            dma("sp", a_rep[:], alog_d.h.ap().partition_broadcast(128), a_rep, alog_d, a_rep)
            dma("sp", dsk[:], dskip_d.h.ap().partition_broadcast(128), dsk, dskip_d, dsk)
            dma("sp", ssdn[:], ssdn_d.h.ap().partition_broadcast(128), ssdn, ssdn_d, ssdn)
            op("act", lambda: S.activation(out=a_rep[:], in_=a_rep[:], func=AF.Exp), reads=[a_rep], writes=[a_rep])
            op("dve", lambda: V.tensor_scalar(out=a_rep[:], in0=a_rep[:], scalar1=-1.0, scalar2=None, op0=ALU.mult), reads=[a_rep], writes=[a_rep])
            S32 = fw.sb(st, [128, 8, 256], F32, "S32")
            Sbf = [fw.sb(st, [128, 8, 256], BF16, "Sbf") for _ in range(2)]
            op("pool", lambda: G.memset(S32[:], 0.0), writes=[S32])
            op("pool", lambda: G.memset(Sbf[0][:], 0.0), writes=[Sbf[0]])
            S32v = [fw.view(S32) for _ in range(4)]
            Sbfv = [[fw.view(Sbf[k]) for _ in range(4)] for k in range(2)]
            for v_ in S32v:
                fw._merge(v_.w, S32.w)
            for v_ in Sbfv[0]:
                fw._merge(v_.w, Sbf[0].w)
            xs_c = [fw.sb(st, [128, 2048], BF16, "xs_c") for _ in range(3)]
            btm_c = [fw.sb(st, [128, 1024], BF16, "btm_c") for _ in range(3)]
            bT_cc = [fw.sb(st, [128, 8, 128], BF16, "bT_c") for _ in range(2)]
            cT_cc = [fw.sb(st, [128, 8, 128], BF16, "cT_c") for _ in range(2)]
            zs_cc = [fw.sb(st, [128, 2048], F32, "zs_c") for _ in range(2)]
            dA = [fw.sb(st, [128, 32], F32, "dA") for _ in range(2)]
            acs = [fw.sb(st, [128, 32], F32, "acs") for _ in range(2)]
            negacs = fw.sb(st, [128, 32], F32, "negacs")
            cd = [fw.sb(st, [128, 32], F32, "cd") for _ in range(2)]
            dte = [fw.sb(st, [128, 32], F32, "dte") for _ in range(2)]
            w1 = [fw.sb(st, [128, 32], F32, "w1") for _ in range(2)]
            Eac = fw.sb(st, [128, 32], F32, "Eac")
            xdd = [fw.sb(st, [128, 2048], BF16, "xdd") for _ in range(2)]
            xd = fw.sb(st, [128, 2048], BF16, "xd")
            dec = [fw.sb(st, [128, 128], F32, "dec") for _ in range(8)]
            GT = [fw.sb(st, [128, 128], BF16, "GT") for _ in range(8)]
            ysb2 = [fw.sb(st, [128, 2048], F32, "ysb") for _ in range(2)]
            ytmp = fw.sb(st, [128, 2048], F32, "ytmp")
            t256 = [fw.sb(st, [128, 256], F32, "t256") for _ in range(2)]
            ssg = fw.sb(st, [128, 8], F32, "ssg")
            ynb = [fw.sb(st, [128, 2048], BF16, "ynb") for _ in range(2)]
            yT = [fw.sb(st, [128, 16, 128], BF16, "yT") for _ in range(2)]
            junk2 = fw.sb(st, [128, 256], F32, "junk2")
            bTv = bT_d.h.rearrange("(g n) t -> n g t", n=128)
            cTv = cT_d.h.rearrange("(g n) t -> n g t", n=128)
            yssdTv = yssdT_d.h.rearrange("(f p) t -> p f t", p=128)

            def loadC(c):
                if c >= NBLK:
                    return
                XS, BTM = xs_c[c % 3], btm_c[c % 3]
                dma("sp", XS[:], xstm_d[c * 128:(c + 1) * 128, :], XS, xstm_d, XS)
                dma("sp", BTM[:], btm_d[c * 128:(c + 1) * 128, :], BTM, btm_d, BTM)
                if c % 2 == 1:
                    i = c // 2
                    dma("sp", bT_cc[i % 2][:], bTv[:, :, c * 128:(c + 1) * 128], bT_cc[i % 2], bT_d, bT_cc[i % 2])
                    dma("sp", cT_cc[i % 2][:], cTv[:, :, c * 128:(c + 1) * 128], cT_cc[i % 2], cT_d, cT_cc[i % 2])
                    dma("sp", zs_cc[i % 2][:], zs_d[i * 128:(i + 1) * 128, :], zs_cc[i % 2], zs_d, zs_cc[i % 2])

            def flushT(i):
                YN, YT = ynb[i % 2], yT[i % 2]
                for half in range(2):
                    bk = bank()
                    pT = bk.hb.rearrange("p (k c) -> p k c", k=8)
                    for j in range(8):
                        f = half * 8 + j
                        op("pe", lambda: PE.transpose(out=pT[:, j, :], in_=YN[:, f * 128:(f + 1) * 128], identity=ident[:]), reads=[YN, ident], writes=[bk])
                    op("act", lambda: S.copy(out=YT[:, half * 8:(half + 1) * 8, :], in_=pT), reads=[bk], writes=[YT])
                dma("sp", yssdTv[:, :, i * 128:(i + 1) * 128], YT[:], yssdT_d, YT, YT)

            hi = 0
            pending = []
            ytmp2 = [ytmp, fw.sb(st, [128, 2048], F32, "ytmp")]
            loadC(0)
            loadC(1)

            def stageA(c):
                own = (c % 2 == 1)
                i = c // 2
                XS, DA, ACS, CD, DTE, W1, XDD = xs_c[c % 3], dA[c % 2], acs[c % 2], cd[c % 2], dte[c % 2], w1[c % 2], xdd[c % 2]
                XS3 = XS.h.rearrange("p (h d) -> p h d", d=64)
                if own:
                    op("pool", lambda: G.tensor_tensor(out=ytmp2[i % 2].h.rearrange("p (h d) -> p h d", d=64), in0=XS3, in1=bc_last(dsk[:], 64), op=ALU.mult), reads=[XS, dsk], writes=[ytmp2[i % 2]])
                op("dve", lambda: V.tensor_tensor(out=DA[:], in0=dtt[:, c, :], in1=a_rep[:], op=ALU.mult), reads=[dtt, a_rep], writes=[DA])
                bA = bank()
                op("pe", lambda: PE.matmul(bA[:, 0:32], lhsT=tri[:], rhs=DA[:], start=True, stop=True), reads=[tri, DA], writes=[bA])
                op("pe", lambda: PE.matmul(bA[:, 32:64], lhsT=onesf[:], rhs=DA[:], start=True, stop=True), reads=[onesf, DA], writes=[bA])
                op("dve", lambda: V.tensor_copy(out=ACS[:], in_=bA[:, 0:32]), reads=[bA], writes=[ACS])
                op("dve", lambda: V.tensor_tensor(out=DTE[:], in0=bA[:, 32:64], in1=ACS[:], op=ALU.subtract), reads=[bA, ACS], writes=[DTE])
                op("act", lambda: S.activation(out=DTE[:], in_=DTE[:], func=AF.Exp), reads=[DTE], writes=[DTE])
                op("act", lambda: S.activation(out=CD[:], in_=bA[:, 32:64], func=AF.Exp), reads=[bA], writes=[CD])
                op("dve", lambda: V.tensor_tensor(out=W1[:], in0=dtt[:, c, :], in1=DTE[:], op=ALU.mult), reads=[dtt, DTE], writes=[W1])
                op("pool", lambda: G.tensor_tensor(out=XDD.h.rearrange("p (h d) -> p h d", d=64), in0=XS3, in1=bc_last(W1[:], 64), op=ALU.mult),
                   reads=[XS, W1], writes=[XDD])
                if own:
                    op("dve", lambda: V.tensor_tensor(out=xd.h.rearrange("p (h d) -> p h d", d=64), in0=XS3, in1=bc_last(dtt[:, c, :], 64), op=ALU.mult),
                       reads=[XS, dtt], writes=[xd])
                    op("dve", lambda: V.tensor_scalar(out=negacs[:], in0=ACS[:], scalar1=-1.0, scalar2=None, op0=ALU.mult), reads=[ACS], writes=[negacs])
                    op("act", lambda: S.activation(out=Eac[:], in_=ACS[:], func=AF.Exp), reads=[ACS], writes=[Eac])

            stageA(0)
            for c in range(NBLK):
                own = (c % 2 == 1)
                i = c // 2
                loadC(c + 2)
                if c + 1 < NBLK:
                    stageA(c + 1)
                BTM, DA, CD, XDD = btm_c[c % 3], dA[c % 2], cd[c % 2], xdd[c % 2]
                bT_c, cT_c, zs_c = bT_cc[i % 2], cT_cc[i % 2], zs_cc[i % 2]
                ysb = ysb2[i % 2]
                ytmp = ytmp2[i % 2]
                Sprev, Snext = Sbf[c % 2], Sbf[(c + 1) % 2]
                if own:
                    gb = {}

                    def G1(g):
                        X1, X2, X3 = bank(), bank(), bank()
                        gb[g] = (X1, X2, X3)
                        op("pe", lambda: PE.matmul(X1[:, 0:128], lhsT=bT_c[:, g, :], rhs=cT_c[:, g, :], start=True, stop=True), reads=[bT_c, cT_c], writes=[X1])
                        for hh in range(4):
                            h = 4 * g + hh
                            o = X2[:, hh * 128:(hh + 1) * 128]
                            op("pe", lambda: PE.matmul(o, lhsT=DA[:, h:h + 1].to_broadcast([128, 128]), rhs=tri[:], start=True, stop=False), reads=[DA, tri], writes=[X2])
                            op("pe", lambda: PE.matmul(o, lhsT=ident[:], rhs=negm_ssd[:], start=False, stop=True), reads=[ident, negm_ssd], writes=[X2])

                    def G2(g):
                        X1, X2, X3 = gb[g]
                        for hh in range(4):
                            h = 4 * g + hh
                            DEC, GTT = dec[(g % 2) * 4 + hh], GT[(g % 2) * 4 + hh]
                            op("act", lambda: S.activation(out=DEC[:], in_=X2[:, hh * 128:(hh + 1) * 128], func=AF.Exp, bias=negacs[:, h:h + 1]), reads=[X2, negacs], writes=[DEC])
                        for hh in range(4):
                            DEC, GTT = dec[(g % 2) * 4 + hh], GT[(g % 2) * 4 + hh]
                            op("dve", lambda: V.tensor_tensor(out=GTT[:], in0=DEC[:], in1=X1[:, 0:128], op=ALU.mult), reads=[DEC, X1], writes=[GTT])

                    def G3(g):
                        X1, X2, X3 = gb.pop(g)
                        op("pe", lambda: PE.matmul(X3[:, 0:256], lhsT=cT_c[:, g, :], rhs=Sprev[:, g, :], start=True, stop=True), reads=[cT_c, Sbfv[c % 2][g // 2]], writes=[X3])
                        for hh in range(4):
                            h = 4 * g + hh
                            GTT = GT[(g % 2) * 4 + hh]
                            op("pe", lambda: PE.matmul(X1[:, 128 + hh * 64:128 + (hh + 1) * 64], lhsT=GTT[:], rhs=xd[:, h * 64:(h + 1) * 64], start=True, stop=True),
                               reads=[GTT, xd], writes=[X1])
                        T2 = t256[g % 2]
                        op("dve", lambda: V.tensor_tensor(out=T2.h.rearrange("p (h d) -> p h d", d=64), in0=X3.h.rearrange("p (h d) -> p h d", d=64)[:, 0:4, :],
                                                          in1=bc_last(Eac[:, 4 * g:4 * g + 4], 64), op=ALU.mult), reads=[X3, Eac], writes=[T2])
                        op("dve", lambda: V.tensor_tensor(out=ysb[:, g * 256:(g + 1) * 256], in0=T2[:], in1=X1[:, 128:384], op=ALU.add), reads=[T2, X1], writes=[ysb])

                    G1(0)
                    for g in range(8):
                        if g >= 1:
                            G3(g - 1)
                        if g + 1 < 8:
                            G1(g + 1)
                        G2(g)
                    G3(7)
                    if i >= 1:
                        flushT(i - 1)
                    def post(ysb=ysb, zs_c=zs_c, i=i, ytmp=ytmp):
                        op("dve", lambda: V.tensor_tensor(out=ysb[:], in0=ysb[:], in1=ytmp[:], op=ALU.add), reads=[ysb, ytmp], writes=[ysb])
                        op("dve", lambda: V.tensor_tensor(out=ysb[:], in0=ysb[:], in1=zs_c[:], op=ALU.mult), reads=[ysb, zs_c], writes=[ysb])
                        for g in range(8):
                            op("act", lambda: S.activation(out=junk2[:], in_=ysb[:, g * 256:(g + 1) * 256], func=AF.Square, accum_out=ssg[:, g:g + 1]), reads=[ysb], writes=[ssg])
                        op("act", lambda: S.activation(out=ssg[:], in_=ssg[:], func=AF.Sqrt, scale=1.0 / 256, bias=EPS), reads=[ssg], writes=[ssg])
                        op("dve", lambda: V.reciprocal(out=ssg[:], in_=ssg[:]), reads=[ssg], writes=[ssg])
                        for g in range(8):
                            op("act", lambda: S.activation(out=ysb[:, g * 256:(g + 1) * 256], in_=ysb[:, g * 256:(g + 1) * 256], func=AF.Identity, scale=ssg[:, g:g + 1]),
                               reads=[ysb, ssg], writes=[ysb])
                        op("dve", lambda: V.tensor_tensor(out=ynb[i % 2][:], in0=ysb[:], in1=ssdn[:], op=ALU.mult), reads=[ysb, ssdn], writes=[ynb[i % 2]])

                    pending.append(post)
                if c < NBLK - 1:
                    for gp in range(4):
                        bS = bank()
                        for gg in range(2):
                            g = gp * 2 + gg
                            op("pe", lambda: PE.matmul(bS[:, gg * 256:(gg + 1) * 256], lhsT=BTM[:, g * 128:(g + 1) * 128], rhs=XDD[:, g * 256:(g + 1) * 256],
                                                       start=True, stop=True), reads=[BTM, XDD], writes=[bS])
                        Sv = S32.h.rearrange("p g (h d) -> p (g h) d", d=64)[:, gp * 8:(gp + 1) * 8, :]
                        op("dve", lambda: V.tensor_tensor(out=Sv, in0=Sv, in1=bc_last(CD[:, gp * 8:(gp + 1) * 8], 64), op=ALU.mult), reads=[S32v[gp], CD], writes=[S32v[gp]])
                        op("dve", lambda: V.tensor_tensor(out=Sv, in0=Sv, in1=bS.h.rearrange("p (h d) -> p h d", d=64), op=ALU.add), reads=[S32v[gp], bS], writes=[S32v[gp]])
                        op("act", lambda: S.copy(out=Snext[:, 2 * gp:2 * gp + 2, :], in_=S32[:, 2 * gp:2 * gp + 2, :]), reads=[S32v[gp]], writes=[Sbfv[(c + 1) % 2][gp]])
                while len(pending) > (1 if own else 0):
                    pending.pop(0)()
            while pending:
                pending.pop(0)()
            flushT(NOB - 1)
            fw.barrier()
        print("phaseC ninst", fw.ninst, "nwaits", fw.nwaits)
        if debug == 2:
            dma("sp", out_d[0:128, :], xs_d[0:128, :], out_d, xs_d, out_d)
            fw.barrier()
            return nc

        stDE = contextlib.ExitStack()
        WE12 = (fw.sb(stDE, [128, 8, 2048], BF16, "Wg"), fw.sb(stDE, [128, 16, 1024], BF16, "Wssd"),
                fw.sb(stDE, [128, 8, 1024], BF16, "Wsb"), fw.sb(stDE, [128, 8, 1024], BF16, "Wout"))
        for w_, src_, nk_ in zip(WE12, (wgate_d, wssd_d, wsb_d, wout_d), (8, 16, 8, 8)):
            sv_ = src_.h.rearrange("(k p) c -> p k c", p=128)
            for kc0 in range(0, nk_, 2):
                dma("pool", w_[:, kc0:kc0 + 2, :], sv_[:, kc0:kc0 + 2, :], w_, src_, w_)
        for src_, dst_ in ((wff1_d, w1_b), (wff2_d, w2_b), (wpleg_d, wpg_b), (wple_d, wp_b)):
            dst_.bg = True
            nr = src_.h.shape[0]
            step = 256 if nr >= 256 else nr
            for r0 in range(0, nr, step):
                dma("pool", dst_[r0:r0 + step, :], src_[r0:r0 + step, :], dst_, src_, dst_)
        with contextlib.ExitStack() as st:
            KT = [fw.sb(st, [128, 2, NPOS], BF16, "KT")] * 2
            QZ = [fw.sb(st, [128, 2, NOB, 2, 128], BF16, "QZ")] * 2
            op("dve", lambda: V.memset(QZ[0][:], 0.0), writes=[QZ[0]])
            Vt = [fw.sb(st, [128, NBLK, 256], BF16, "Vt")] * 2
            e_p = [fw.sb(st, [128, 1024], F32, "e_p") for _ in range(2)]
            L_p = [fw.sb(st, [128, 1024], BF16, "L_p") for _ in range(4)]
            w_p = [fw.sb(st, [128, 1024], BF16, "w_p") for _ in range(3)]
            Suf32 = [fw.sb(st, [128, 512], F32, "Suf32") for _ in range(2)]
            SufB = [fw.sb(st, [128, 512], BF16, "SufB") for _ in range(2)]
            yst = [fw.sb(st, [128, 2, NOWN], BF16, "yst") for _ in range(2)]
            kTv = kT_d.h.rearrange("(f p) t -> p f t", p=128)
            qTv = qT_d.h.rearrange("(f p) t -> p f t", p=128)
            vv = v_d.h.rearrange("(b p) c -> p b c", p=128)
            ysbTv = ysbT_d.h.rearrange("(f p) t -> p f t", p=128)
            units = []
            for bq in range(4):
                for i in range(NOB):
                    for u, kb in enumerate(range(2 * i + 1, -1, -1)):
                        units.append((bq, i, kb, u == 0, kb == 0))
            NU = len(units)
            NP = NU // 2
            OB = [banks[0], banks[1]]

            def Xb(n):
                return banks[2 + 2 * ((n // 2) % 3) + n % 2]

            def Xpair(pp):
                k = 2 + 2 * (pp % 3)
                return ps_all[:, k * 512:(k + 2) * 512]

            def Lu(n):
                return L_p[(n // 2) % 4][:, (n % 2) * 512:(n % 2 + 1) * 512]

            def load_KQ(bq):
                K_, Q_ = KT[0], QZ[0]
                dma("sp", K_[:], kTv[:, 2 * bq:2 * bq + 2, :], K_, kT_d, K_)
                for pr in range(2):
                    for hb in range(2):
                        r0 = (2 * bq + pr) * 128 + hb * 64
                        dma("sp", Q_[hb * 64:(hb + 1) * 64, pr, :, hb, :], qT_d.h.rearrange("r (i t) -> r i t", t=128)[r0:r0 + 64], Q_, qT_d, Q_)

            def load_V(bq):
                dma("sp", Vt[0][:], vv[:, :, bq * 256:(bq + 1) * 256], Vt[0], v_d, Vt[0])

            def S1(pp):
                EP, LP = e_p[pp % 2], L_p[pp % 4]
                for n in (2 * pp, 2 * pp + 1):
                    bq, i, kb, first, last = units[n]
                    if i == 0 and first:
                        load_KQ(bq)
                    K_, Q_, X = KT[0], QZ[0], Xb(n)
                    for pr in range(2):
                        op("pe", lambda: PE.matmul(X[:, pr * 256:(pr + 1) * 256], lhsT=K_[:, pr, kb * 128:(kb + 1) * 128],
                                                   rhs=Q_[:, pr, i, :, :], start=(pr == 0), stop=(pr == 1 and not first)), reads=[K_, Q_], writes=[X])
                    if first:
                        op("pe", lambda: PE.matmul(X[:, :], lhsT=ident[:], rhs=negm[:], start=False, stop=True), reads=[ident, negm], writes=[X])
                xs_ = [Xb(2 * pp), Xb(2 * pp + 1)]
                op("act", lambda: S.activation(out=EP[:], in_=Xpair(pp), func=AF.Exp), reads=xs_, writes=[EP])
                op("act", lambda: S.activation(out=LP[:], in_=EP[:], func=AF.Ln, bias=1.0), reads=[EP], writes=[LP])

            def S2(n):
                bq, i, kb, first, last = units[n]
                if last:
                    return
                LP, SF, SBn = L_p[(n // 2) % 4], Suf32[i % 2], SufB[(n + 1) % 2]
                if first:
                    op("dve", lambda: V.tensor_copy(out=SF[:], in_=Lu(n)), reads=[LP], writes=[SF])
                    op("dve", lambda: V.tensor_copy(out=SBn[:], in_=Lu(n)), reads=[LP], writes=[SBn])
                else:
                    op("dve", lambda: V.tensor_tensor(out=SF[:], in0=SF[:], in1=Lu(n), op=ALU.add), reads=[SF, LP], writes=[SF])
                    op("dve", lambda: V.tensor_copy(out=SBn[:], in_=SF[:]), reads=[SF], writes=[SBn])

            def S3a_mm(n):
                bq, i, kb, first, last = units[n]
                X, LP, SBp = Xb(n), L_p[(n // 2) % 4], SufB[n % 2]
                op("pe", lambda: PE.matmul(X[:, :], lhsT=negU[:], rhs=Lu(n), start=False, stop=first, skip_group_check=True), reads=[negU, LP], writes=[X])
                if not first:
                    op("pe", lambda: PE.matmul(X[:, :], lhsT=negO[:], rhs=SBp[:], start=False, stop=True, skip_group_check=True), reads=[negO, SBp], writes=[X])

            def S3a_act(pp):
                WP = w_p[pp % 3]
                op("act", lambda: S.activation(out=WP[:], in_=Xpair(pp), func=AF.Exp), reads=[Xb(2 * pp), Xb(2 * pp + 1)], writes=[WP])

            def S3b(n):
                bq, i, kb, first, last = units[n]
                if i == 0 and first:
                    load_V(bq)
                V_, YS, WP = Vt[0], yst[bq % 2], w_p[(n // 2) % 3]
                c0 = (n % 2) * 512
                for j in range(4):
                    pr, hb = j // 2, j % 2
                    op("pe", lambda: PE.matmul(OB[pr][hb * 64:(hb + 1) * 64, 0:128], lhsT=V_[:, kb, j * 64:(j + 1) * 64], rhs=WP[:, c0 + j * 128:c0 + (j + 1) * 128],
                                               start=first, stop=last), reads=[V_, WP], writes=[OB[pr]])
                if last:
                    for pr in range(2):
                        op("dve", lambda: V.tensor_copy(out=YS[:, pr, i * 128:(i + 1) * 128], in_=OB[pr][:, 0:128]), reads=[OB[pr]], writes=[YS])
                    if i == NOB - 1:
                        dma("sp", ysbTv[:, 2 * bq:2 * bq + 2, :], YS[:], ysbT_d, YS, YS)

            S1(0)
            S1(1)
            for pp in range(NP):
                S2(2 * pp)
                S3a_mm(2 * pp)
                S3a_mm(2 * pp + 1)
                S3a_act(pp)
                if pp + 2 < NP:
                    S1(pp + 2)
                S2(2 * pp + 1)
                if pp >= 1:
                    S3b(2 * pp - 2)
                    S3b(2 * pp - 1)
            S3b(NU - 2)
            S3b(NU - 1)
            fw.barrier()
        print("phaseD ninst", fw.ninst, "nwaits", fw.nwaits)
        if debug == 3:
            dma("sp", out_d[0:128, :], xs_d[0:128, :], out_d, xs_d, out_d)
            fw.barrier()
            stDE.close()
            return nc

        def rep_load(stk, src_d, n, name):
            t = fw.sb(stk, [128, n], F32, name)
            dma("sp", t[:], src_d.h.ap().partition_broadcast(128), t, src_d, t)
            return t

        def norm_residual(ss2, junkE, bks, rep, resid, dst):
            for half in range(2):
                op("act", lambda: S.activation(out=junkE[:, 0:512], in_=bks[half][:, :], func=AF.Square, accum_out=ss2[:, half:half + 1]), reads=[bks[half]], writes=[ss2])
            op("dve", lambda: V.tensor_tensor(out=ss2[:, 2:3], in0=ss2[:, 0:1], in1=ss2[:, 1:2], op=ALU.add), reads=[ss2], writes=[ss2])
            op("act", lambda: S.activation(out=ss2[:, 2:3], in_=ss2[:, 2:3], func=AF.Sqrt, scale=1.0 / 1024, bias=EPS), reads=[ss2], writes=[ss2])
            op("dve", lambda: V.reciprocal(out=ss2[:, 2:3], in_=ss2[:, 2:3]), reads=[ss2], writes=[ss2])
            for half in range(2):
                op("dve", lambda: V.scalar_tensor_tensor(out=dst[:, half * 512:(half + 1) * 512], in0=bks[half][:, :], scalar=ss2[:, 2:3],
                                                         in1=rep[:, half * 512:(half + 1) * 512], op0=ALU.mult, op1=ALU.mult), reads=[bks[half], ss2, rep], writes=[dst])
            op("dve", lambda: V.tensor_tensor(out=dst[:], in0=dst[:], in1=resid[:], op=ALU.add), reads=[dst, resid], writes=[dst])

        def transposes_to(src16, bk, dst_fn):
            pT = bk.hb.rearrange("p (k c) -> p k c", k=8)
            for k in range(8):
                op("pe", lambda: PE.transpose(out=pT[:, k, :], in_=src16[:, k * 128:(k + 1) * 128], identity=ident[:]), reads=[src16, ident], writes=[bk])
            dst_fn(pT)

        with contextlib.ExitStack() as st:
            Wg, Wssd, Wsb, Wout = WE12
            bg = fw.sb(st, [128, 16], F32, "bg")
            load_fm(st, bg, 0, bgate_d.h.rearrange("(f p) -> f p", p=128), 16, bgate_d)
            gfm2 = fw.sb(st, [128, 8], F32, "gfm2")
            load_fm(st, gfm2, 0, nfpre_d.h.rearrange("(k p) -> k p", p=128), 8, nfpre_d)
            nmpost = rep_load(st, nmpost_d, 1024, "nmpost")
            n1cE = [fw.sb(st, [128, 8, 512], BF16, "n1cE")] * 2
            yssE = [fw.sb(st, [128, 16, 512], BF16, "yssE")] * 2
            ysbE = [fw.sb(st, [128, 8, 512], BF16, "ysbE") for _ in range(2)]
            mTt = [fw.sb(st, [128, 8, 512], BF16, "mT")] * 2
            gsT = [fw.sb(st, [128, 512], F32, "gsT") for _ in range(2)]
            gbT = [fw.sb(st, [128, 512], F32, "gbT") for _ in range(2)]
            m1T = [fw.sb(st, [128, 512], F32, "m1T") for _ in range(2)]
            m2T = [fw.sb(st, [128, 512], F32, "m2T") for _ in range(2)]
            xb = [fw.sb(st, [128, 1024], F32, "xb") for _ in range(2)]
            h1t = [fw.sb(st, [128, 1024], F32, "h1t") for _ in range(2)]
            xn2 = [fw.sb(st, [128, 1024], BF16, "xn2") for _ in range(2)]
            n2st = [fw.sb(st, [128, 8, 128], BF16, "n2st") for _ in range(2)]
            ss2 = [fw.sb(st, [128, 4], F32, "ss2") for _ in range(2)]
            junkE = fw.sb(st, [128, 1024], F32, "junkE")
            n1Tov = n1To_d.h.rearrange("(k p) t -> p k t", p=128)
            yssv = yssdT_d.h.rearrange("(k p) t -> p k t", p=128)
            ysbv = ysbT_d.h.rearrange("(k p) t -> p k t", p=128)
            n2Tv = n2T_d.h.rearrange("(k p) t -> p k t", p=128)
            pend = []

            def loadE(tc):
                if tc >= 4:
                    return
                sl = slice(tc * 512, (tc + 1) * 512)
                dma("sp", n1cE[tc % 2][:], n1Tov[:, :, sl], n1cE[tc % 2], n1To_d, n1cE[tc % 2])
                dma("sp", yssE[tc % 2][:], yssv[:, :, sl], yssE[tc % 2], yssdT_d, yssE[tc % 2])
                dma("sp", ysbE[tc % 2][:], ysbv[:, :, sl], ysbE[tc % 2], ysbT_d, ysbE[tc % 2])

            def Bm(tc, tb):
                blk = 4 * tc + tb
                mT = mTt[tc % 2]
                bks = [banks[(blk % 2) * 2], banks[(blk % 2) * 2 + 1]]
                dma("sp", xb[blk % 2][:], xs_d[(2 * blk + 1) * 128:(2 * blk + 2) * 128, :], xb[blk % 2], xs_d, xb[blk % 2])
                for half in range(2):
                    for kc in range(8):
                        op("pe", lambda: PE.matmul(bks[half][:, :], lhsT=mT[:, kc, tb * 128:(tb + 1) * 128], rhs=Wout[:, kc, half * 512:(half + 1) * 512],
                                                   start=(kc == 0), stop=(kc == 7)), reads=[mT, Wout], writes=[bks[half]])

            def Bn(tc, tb):
                blk = 4 * tc + tb
                XB, H1, XN2, SS = xb[blk % 2], h1t[blk % 2], xn2[blk % 2], ss2[blk % 2]
                bks = [banks[(blk % 2) * 2], banks[(blk % 2) * 2 + 1]]
                norm_residual(SS, junkE, bks, nmpost, XB, H1)
                dma("sp", h1_d[blk * 128:(blk + 1) * 128, :], H1[:], h1_d, H1, H1)
                op("dve", lambda: V.scalar_tensor_tensor(out=junkE[:], in0=H1[:], scalar=1.0, in1=H1[:], op0=ALU.mult, op1=ALU.mult, accum_out=SS[:, 3:4]), reads=[H1], writes=[SS])
                op("act", lambda: S.activation(out=SS[:, 3:4], in_=SS[:, 3:4], func=AF.Sqrt, scale=1.0 / 1024, bias=EPS), reads=[SS], writes=[SS])
                op("dve", lambda: V.reciprocal(out=SS[:, 3:4], in_=SS[:, 3:4]), reads=[SS], writes=[SS])
                op("act", lambda: S.activation(out=XN2[:], in_=H1[:], func=AF.Identity, scale=SS[:, 3:4]), reads=[H1, SS], writes=[XN2])

            def Bt(tc, tb):
                blk = 4 * tc + tb
                XN2, N2S = xn2[blk % 2], n2st[blk % 2]
                bk = banks[(blk % 2) * 2]
                transposes_to(XN2, bk, lambda pT: op("dve", lambda: V.tensor_tensor(out=N2S[:], in0=pT, in1=bc_last(gfm2[:], 128), op=ALU.mult), reads=[bk, gfm2], writes=[N2S]))
                dma("sp", n2Tv[:, :, blk * 128:(blk + 1) * 128], N2S[:], n2T_d, N2S, N2S)

            for tc in range(4):
                loadE(tc)
                N1, YSS, YSB, mT = n1cE[tc % 2], yssE[tc % 2], ysbE[tc % 2], mTt[tc % 2]
                for of in range(8):
                    GS, GB, M1, M2 = gsT[of % 2], gbT[of % 2], m1T[of % 2], m2T[of % 2]
                    b1, b2, b3, b4 = banks[4], banks[5], banks[6], banks[7]
                    for kc in range(8):
                        op("pe", lambda: PE.matmul(b1[:, :], lhsT=Wg[:, kc, of * 128:(of + 1) * 128], rhs=N1[:, kc, :], start=(kc == 0), stop=(kc == 7)), reads=[Wg, N1], writes=[b1])
                    op("act", lambda: S.activation(out=GS[:], in_=b1[:, :], func=AF.Sigmoid, bias=bg[:, of:of + 1]), reads=[b1, bg], writes=[GS])
                    for kc in range(8):
                        op("pe", lambda: PE.matmul(b2[:, :], lhsT=Wg[:, kc, 1024 + of * 128:1024 + (of + 1) * 128], rhs=N1[:, kc, :], start=(kc == 0), stop=(kc == 7)), reads=[Wg, N1], writes=[b2])
                    op("act", lambda: S.activation(out=GB[:], in_=b2[:, :], func=AF.Sigmoid, bias=bg[:, 8 + of:9 + of]), reads=[b2, bg], writes=[GB])
                    for kc in range(16):
                        op("pe", lambda: PE.matmul(b3[:, :], lhsT=Wssd[:, kc, of * 128:(of + 1) * 128], rhs=YSS[:, kc, :], start=(kc == 0), stop=(kc == 15)), reads=[Wssd, YSS], writes=[b3])
                    op("dve", lambda: V.tensor_tensor(out=M1[:], in0=GS[:], in1=b3[:, :], op=ALU.mult), reads=[GS, b3], writes=[M1])
                    for kc in range(8):
                        op("pe", lambda: PE.matmul(b4[:, :], lhsT=Wsb[:, kc, of * 128:(of + 1) * 128], rhs=YSB[:, kc, :], start=(kc == 0), stop=(kc == 7)), reads=[Wsb, YSB], writes=[b4])
                    op("dve", lambda: V.tensor_tensor(out=M2[:], in0=GB[:], in1=b4[:, :], op=ALU.mult), reads=[GB, b4], writes=[M2])
                    op("dve", lambda: V.tensor_tensor(out=mT[:, of, :], in0=M1[:], in1=M2[:], op=ALU.add), reads=[M1, M2], writes=[mT])
                    if pend:
                        pend.pop(0)()
                while pend:
                    pend.pop(0)()
                for tb in range(4):
                    if tb >= 2:
                        Bt(tc, tb - 2)
                    Bm(tc, tb)
                    if tb >= 1:
                        Bn(tc, tb - 1)
                pend.append(lambda tc=tc: Bn(tc, 3))
                pend.append(lambda tc=tc: Bt(tc, 2))
                pend.append(lambda tc=tc: Bt(tc, 3))
            while pend:
                pend.pop(0)()
            fw.barrier()
        stDE.close()
        print("phaseE12 ninst", fw.ninst, "nwaits", fw.nwaits)

        with contextlib.ExitStack() as st:
            W1 = fw.sb(st, [128, 8, 4096], BF16, "W1")
            W2 = fw.sb(st, [128, 32, 1024], BF16, "W2")
            W1v = [fw.view(W1) for _ in range(8)]
            W2v = [fw.view(W2) for _ in range(4)]
            w1bv = w1_b.h.rearrange("(k p) c -> p k c", p=128)
            w2bv = w2_b.h.rearrange("(k p) c -> p k c", p=128)
            n2c = [fw.sb(st, [128, 8, 512], BF16, "n2c")] * 2
            n2Tv = n2T_d.h.rearrange("(k p) t -> p k t", p=128)
            h2Tv = h2T_d.h.rearrange("(k p) t -> p k t", p=128)
            dma("sp", n2c[0][:], n2Tv[:, :, 0:512], n2c[0], n2T_d, n2c[0])
            for j in range(8):
                dma("sp" if j % 2 == 0 else "act", W1[:, :, j * 512:(j + 1) * 512], w1bv[:, :, j * 512:(j + 1) * 512], W1v[j], w1_b, W1v[j])
            for j in range(4):
                dma("sp" if j % 2 == 0 else "act", W2[:, j * 8:(j + 1) * 8, :], w2bv[:, j * 8:(j + 1) * 8, :], W2v[j], w2_b, W2v[j])
            nfpost = rep_load(st, nfpost_d, 1024, "nfpost")
            aTt = fw.sb(st, [128, 32, 512], BF16, "aT")
            r_t = [fw.sb(st, [128, 512], F32, "r_t") for _ in range(2)] + [None]
            h1b = [fw.sb(st, [128, 1024], F32, "h1b") for _ in range(2)]
            h2t = [fw.sb(st, [128, 1024], F32, "h2t")] * 2
            hb16 = fw.sb(st, [128, 1024], BF16, "hb16")
            h2st = fw.sb(st, [128, 8, 128], BF16, "h2st")
            ss2 = [fw.sb(st, [128, 4], F32, "ss2b") for _ in range(2)]
            junkE = fw.sb(st, [128, 512], F32, "junkE2")
            pend = []

            def Fm(tc, tb):
                blk = 4 * tc + tb
                bks = [banks[(blk % 2) * 2], banks[(blk % 2) * 2 + 1]]
                dma("sp", h1b[blk % 2][:], h1_d[blk * 128:(blk + 1) * 128, :], h1b[blk % 2], h1_d, h1b[blk % 2])
                for half in range(2):
                    for kc in range(32):
                        op("pe", lambda: PE.matmul(bks[half][:, :], lhsT=aTt[:, kc, tb * 128:(tb + 1) * 128], rhs=W2[:, kc, half * 512:(half + 1) * 512],
                                                   start=(kc == 0), stop=(kc == 31)), reads=[aTt, W2v[kc // 8]], writes=[bks[half]])

            def Fn(tc, tb):
                blk = 4 * tc + tb
                H1B, H2, SS = h1b[blk % 2], h2t[blk % 2], ss2[blk % 2]
                bks = [banks[(blk % 2) * 2], banks[(blk % 2) * 2 + 1]]
                norm_residual(SS, junkE, bks, nfpost, H1B, H2)
                dma("sp", h2_d[blk * 128:(blk + 1) * 128, :], H2[:], h2_d, H2, H2)
                op("act", lambda: S.copy(out=hb16[:], in_=H2[:]), reads=[H2], writes=[hb16])

            def Ft(tc, tb):
                blk = 4 * tc + tb
                bk = banks[(blk % 2) * 2]
                transposes_to(hb16, bk, lambda pT: op("act", lambda: S.copy(out=h2st[:], in_=pT), reads=[bk], writes=[h2st]))
                dma("sp", h2Tv[:, :, blk * 128:(blk + 1) * 128], h2st[:], h2T_d, h2st, h2st)

            hfi = 0
            for tc in range(4):
                N2 = n2c[tc % 2]
                if tc >= 1:
                    dma("sp", N2[:], n2Tv[:, :, tc * 512:(tc + 1) * 512], N2, n2T_d, N2)
                for hf in range(32):
                    RT = r_t[hfi % 2]
                    bk = banks[4 + hfi % 4]
                    hfi += 1
                    for kc in range(8):
                        op("pe", lambda: PE.matmul(bk[:, :], lhsT=W1[:, kc, hf * 128:(hf + 1) * 128], rhs=N2[:, kc, :], start=(kc == 0), stop=(kc == 7)), reads=[W1v[hf // 4], N2], writes=[bk])
                    op("act", lambda: S.activation(out=RT[:], in_=bk[:, :], func=AF.Relu), reads=[bk], writes=[RT])
                    op("dve", lambda: V.tensor_tensor(out=aTt[:, hf, :], in0=RT[:], in1=RT[:], op=ALU.mult), reads=[RT], writes=[aTt])
                    if pend and hf % 4 == 3:
                        pend.pop(0)()
                while pend:
                    pend.pop(0)()
                for tb in range(4):
                    Fm(tc, tb)
                    if tb >= 1:
                        Fn(tc, tb - 1)
                        Ft(tc, tb - 1)
                pend.append(lambda tc=tc: Fn(tc, 3))
                pend.append(lambda tc=tc: Ft(tc, 3))
            while pend:
                pend.pop(0)()
            fw.barrier()
        print("phaseE34 ninst", fw.ninst, "nwaits", fw.nwaits)

        with contextlib.ExitStack() as st:
            Wpg = fw.sb(st, [128, 8, 1024], BF16, "Wpg")
            Wp = fw.sb(st, [128, 2, 1024], BF16, "Wp")
            dma("sp", Wpg[:], wpg_b.h.rearrange("(k p) c -> p k c", p=128), Wpg, wpg_b, Wpg)
            dma("act", Wp[:], wp_b.h.rearrange("(k p) c -> p k c", p=128), Wp, wp_b, Wp)
            nppost = rep_load(st, nppost_d, 1024, "nppost")
            h2b = [fw.sb(st, [128, 1024], F32, "h2b") for _ in range(3)]
            h2Tb = [fw.sb(st, [128, 8, 128], BF16, "h2Tb") for _ in range(3)]
            pbt = [fw.sb(st, [128, 256], F32, "pbt") for _ in range(3)]
            pb16 = [fw.sb(st, [128, 256], BF16, "pb16") for _ in range(2)]
            pTt = [fw.sb(st, [128, 2, 128], BF16, "pTt") for _ in range(2)]
            sg = [fw.sb(st, [128, 1024], F32, "sg") for _ in range(2)]
            prod = [fw.sb(st, [128, 1024], F32, "prod") for _ in range(2)]
            ot = [fw.sb(st, [128, 1024], F32, "ot") for _ in range(2)]
            ss1 = [fw.sb(st, [128, 1], F32, "ss1") for _ in range(2)]
            junkE = fw.sb(st, [128, 1024], F32, "junkE3")
            h2Tv = h2T_d.h.rearrange("(k p) t -> p k t", p=128)

            def loadP(blk):
                if blk >= NOB:
                    return
                dma("sp", h2b[blk % 3][:], h2_d[blk * 128:(blk + 1) * 128, :], h2b[blk % 3], h2_d, h2b[blk % 3])
                dma("sp", h2Tb[blk % 3][:], h2Tv[:, :, blk * 128:(blk + 1) * 128], h2Tb[blk % 3], h2T_d, h2Tb[blk % 3])
                dma("sp", pbt[blk % 3][:], p_d[blk * 128:(blk + 1) * 128, :], pbt[blk % 3], p_d, pbt[blk % 3])

            def Pm(blk):
                H2TB, PB, PB16, PT, SG, PR = h2Tb[blk % 3], pbt[blk % 3], pb16[blk % 2], pTt[blk % 2], sg[blk % 2], prod[blk % 2]
                op("pool", lambda: G.tensor_copy(out=PB16[:], in_=PB[:]), reads=[PB], writes=[PB16])
                bk = bank()
                pT = bk.hb.rearrange("p (k c) -> p k c", k=8)
                for k in range(2):
                    op("pe", lambda: PE.transpose(out=pT[:, k, :], in_=PB16[:, k * 128:(k + 1) * 128], identity=ident[:]), reads=[PB16, ident], writes=[bk])
                op("act", lambda: S.copy(out=PT[:], in_=pT[:, 0:2, :]), reads=[bk], writes=[PT])
                for half in range(2):
                    hs = slice(half * 512, (half + 1) * 512)
                    bg_, bp_ = bank(), bank()
                    for kc in range(8):
                        op("pe", lambda: PE.matmul(bg_[:, :], lhsT=H2TB[:, kc, :], rhs=Wpg[:, kc, hs], start=(kc == 0), stop=(kc == 7)), reads=[H2TB, Wpg], writes=[bg_])
                    op("act", lambda: S.activation(out=SG[:, hs], in_=bg_[:, :], func=AF.Sigmoid), reads=[bg_], writes=[SG])
                    for kc in range(2):
                        op("pe", lambda: PE.matmul(bp_[:, :], lhsT=PT[:, kc, :], rhs=Wp[:, kc, hs], start=(kc == 0), stop=(kc == 1)), reads=[PT, Wp], writes=[bp_])
                    op("dve", lambda: V.tensor_tensor(out=PR[:, hs], in0=SG[:, hs], in1=bp_[:, :], op=ALU.mult), reads=[SG, bp_], writes=[PR])

            def Pn(blk):
                H2B, PR, OT, SS = h2b[blk % 3], prod[blk % 2], ot[blk % 2], ss1[blk % 2]
                op("dve", lambda: V.scalar_tensor_tensor(out=junkE[:], in0=PR[:], scalar=1.0, in1=PR[:], op0=ALU.mult, op1=ALU.mult, accum_out=SS[:]), reads=[PR], writes=[SS])
                op("act", lambda: S.activation(out=SS[:], in_=SS[:], func=AF.Sqrt, scale=1.0 / 1024, bias=EPS), reads=[SS], writes=[SS])
                op("dve", lambda: V.reciprocal(out=SS[:], in_=SS[:]), reads=[SS], writes=[SS])
                op("dve", lambda: V.scalar_tensor_tensor(out=OT[:], in0=PR[:], scalar=SS[:], in1=nppost[:], op0=ALU.mult, op1=ALU.mult), reads=[PR, SS, nppost], writes=[OT])
                op("dve", lambda: V.tensor_tensor(out=OT[:], in0=OT[:], in1=H2B[:], op=ALU.add), reads=[OT, H2B], writes=[OT])
                dma("sp", out_d[blk * 128:(blk + 1) * 128, :], OT[:], out_d, OT, OT)

            loadP(0)
            loadP(1)
            for blk in range(NOB):
                if blk >= 1:
                    Pn(blk - 1)
                    loadP(blk + 1)
                Pm(blk)
            Pn(NOB - 1)
            fw.barrier(recycle=False)
        print("total ninst", fw.ninst, "nwaits", fw.nwaits, "nsems", 5 + len(fw.dma_T))
    return nc

    return nc


def host_inputs(inputs):
    x = np.asarray(inputs["x"], dtype=np.float32)
    p = np.asarray(inputs["p"], dtype=np.float32)[0]
    shared = {}
    for k in ("norm_mix_pre", "w_in", "conv_w", "conv_b", "dt_bias", "a_log", "d_skip", "ssd_norm", "w_ssd_branch",
              "w_sb_branch", "w_gate", "b_gate", "w_out", "norm_mix_post", "norm_ffn_pre", "w_ff1", "w_ff2",
              "norm_ffn_post", "w_ple", "w_ple_gate", "norm_ple_post"):
        shared[k] = np.ascontiguousarray(np.asarray(inputs[k], dtype=np.float32)[0])
    maps = []
    for c in range(8):
        b, r = c // 2, c % 2
        xs = np.zeros((NPOS, 1024), np.float32)
        if r == 0:
            xs[128:] = x[b, :NPOS - 128]
        else:
            xs[:] = x[b]
        own = p[b].reshape(NBLK, 128, 256)[r::2].reshape(NOWN, 256)
        valid = np.ones((128, NBLK), np.float32)
        if r == 0:
            valid[:, 0] = 0.0
        m = dict(shared)
        m["xs"] = xs
        m["p_own"] = np.ascontiguousarray(own)
        m["valid"] = valid
        maps.append(m)
    return maps


def kernel(**inputs):
    nc = build()
    maps = host_inputs(inputs)
    res = run_bass_kernel_spmd(nc, maps, core_ids=list(range(8)))
    out = np.zeros((4, NBLK, 128, 1024), np.float32)
    for c in range(8):
        b, r = c // 2, c % 2
        out[b, r::2] = np.asarray(res.results[c]["out"], dtype=np.float32).reshape(NOB, 128, 1024)
    return out.reshape(4, 4096, 1024)
```

```python
import contextlib
import numpy as np
import concourse.bass as bass
import concourse.mybir as mybir
from concourse.bass_utils import run_bass_kernel_spmd

F32 = mybir.dt.float32
BF16 = mybir.dt.bfloat16
AF = mybir.ActivationFunctionType
ALU = mybir.AluOpType

NPOS = 4096
NBLK = 32
NOWN = 2048
NOB = 16
EPS = 1e-6
OFF_Z, OFF_X, OFF_B, OFF_C, OFF_DT, OFF_Q, OFF_K, OFF_V = 0, 2048, 4096, 5120, 6144, 6176, 7200, 8224
NEG = -30000.0


ALL_T = []


class T:
    __slots__ = ("h", "hb", "w", "r", "sem", "cnt", "name", "bg", "keep")

    def __init__(self, h, name="", hb=None):
        self.h = h
        self.hb = hb
        self.w = {}
        self.r = {}
        self.sem = None
        self.cnt = 0
        self.name = name
        self.bg = False
        self.keep = False
        ALL_T.append(self)

    def __getitem__(self, k):
        return self.h[k]


class FW:
    def __init__(self, nc):
        self.nc = nc
        self.engs = {"pe": nc.tensor, "dve": nc.vector, "act": nc.scalar,
                     "pool": nc.gpsimd, "sp": nc.sync}
        self.sem = {k: nc.alloc_semaphore(name="sem_" + k) for k in self.engs}
        self.cnt = {k: 0 for k in self.engs}
        self.waited = {k: {} for k in self.engs}
        self.semobj = {id(s): s for s in self.sem.values()}
        self.dma_T = []
        self.free_sems = []
        self.ninst = 0
        self.nwaits = 0
        self.uid = 0

    def _name(self, p):
        self.uid += 1
        return f"{p}{self.uid}"

    def sb(self, stack, shape, dt, name=None):
        name = self._name(name or "sb")
        h = stack.enter_context(self.nc.sbuf_tensor(name, list(shape), dt))
        return T(h, name)

    def view(self, t):
        return T(t.h, self._name(t.name + "_v"))

    def dram(self, name, shape, dt, kind="Internal"):
        return T(self.nc.dram_tensor(name, list(shape), dt, kind=kind), name)

    def _wait(self, e, deps):
        eng = self.engs[e]
        wd = self.waited[e]
        own = id(self.sem[e])
        for sid, val in deps.items():
            if val <= 0 or wd.get(sid, 0) >= val:
                continue
            if e == "pe" and sid == own:
                continue
            eng.wait_ge(self.semobj[sid], val)
            wd[sid] = val
            self.nwaits += 1

    @staticmethod
    def _merge(d, s):
        for k, v in s.items():
            if d.get(k, 0) < v:
                d[k] = v

    def op(self, e, fn, reads=(), writes=()):
        deps = {}
        for t in reads:
            self._merge(deps, t.w)
        for t in writes:
            self._merge(deps, t.w)
            self._merge(deps, t.r)
        self._wait(e, deps)
        ins = fn()
        self.cnt[e] += 1
        self.ninst += 1
        ins.then_inc(self.sem[e], 1)
        d = {id(self.sem[e]): self.cnt[e]}
        for t in reads:
            self._merge(t.r, d)
        for t in writes:
            self._merge(t.w, d)

    def dma(self, q, out_ap, in_ap, dst, src, own, **kw):
        deps = {}
        self._merge(deps, src.w)
        self._merge(deps, dst.w)
        self._merge(deps, dst.r)
        self._wait(q, deps)
        if own.sem is None:
            if q == "pool":
                own.keep = True
            if self.free_sems and not own.keep:
                own.sem, own.cnt = self.free_sems.pop()
            else:
                own.sem = self.nc.alloc_semaphore(name=f"ds{len(self.semobj)}")
                own.cnt = 0
            self.semobj[id(own.sem)] = own.sem
            self.dma_T.append(own)
        ins = self.engs[q].dma_start(out=out_ap, in_=in_ap, **kw)
        own.cnt += 16
        self.ninst += 1
        ins.then_inc(own.sem, 16)
        d = {id(own.sem): own.cnt}
        self._merge(src.r, d)
        self._merge(dst.w, d)

    def barrier(self, recycle=True):
        deps = {id(self.sem[k]): self.cnt[k] for k in self.engs if k != "sp"}
        fg = [t for t in self.dma_T if not t.bg]
        for t in fg:
            deps[id(t.sem)] = t.cnt
        self._wait("sp", deps)
        dead = set()
        rec = [t for t in fg if not t.keep] if recycle else []
        for t in rec:
            dead.add(id(t.sem))
        self.engs["sp"].sem_inc(self.sem["sp"], 1)
        self.cnt["sp"] += 1
        d = {id(self.sem["sp"]): self.cnt["sp"]}
        for k in self.engs:
            if k != "sp":
                self._wait(k, d)
        if recycle:
            for t in ALL_T:
                for k in dead:
                    t.w.pop(k, None)
                    t.r.pop(k, None)
            for k in self.engs:
                for sid in dead:
                    self.waited[k].pop(sid, None)
            for t in rec:
                self.free_sems.append((t.sem, t.cnt))
                self.dma_T.remove(t)
                t.sem = None
                t.cnt = 0


def bc_last(ap, n):
    sh = list(ap.shape)
    return ap.unsqueeze(len(sh)).to_broadcast(sh + [n])


def build(debug=0):
    del ALL_T[:]
    nc = bass.Bass("TRN2", target_bir_lowering=False)
    fw = FW(nc)
    op, dma = fw.op, fw.dma
    V, S, P, G, PE = nc.vector, nc.scalar, nc.gpsimd, nc.gpsimd, nc.tensor
    dk = "ExternalOutput" if debug else "Internal"

    def din(name, shape):
        return fw.dram(name, shape, F32, kind="ExternalInput")

    xs_d = din("xs", [NPOS, 1024])
    p_d = din("p_own", [NOWN, 256])
    valid_d = din("valid", [128, NBLK])
    nmpre_d = din("norm_mix_pre", [1024])
    win_d = din("w_in", [1024, 9248])
    convw_d = din("conv_w", [4, 4096])
    convb_d = din("conv_b", [4096])
    dtb_d = din("dt_bias", [32])
    alog_d = din("a_log", [32])
    dskip_d = din("d_skip", [32])
    ssdn_d = din("ssd_norm", [2048])
    wssd_d = din("w_ssd_branch", [2048, 1024])
    wsb_d = din("w_sb_branch", [1024, 1024])
    wgate_d = din("w_gate", [1024, 2048])
    bgate_d = din("b_gate", [2048])
    wout_d = din("w_out", [1024, 1024])
    nmpost_d = din("norm_mix_post", [1024])
    nfpre_d = din("norm_ffn_pre", [1024])
    wff1_d = din("w_ff1", [1024, 4096])
    wff2_d = din("w_ff2", [4096, 1024])
    nfpost_d = din("norm_ffn_post", [1024])
    wple_d = din("w_ple", [256, 1024])
    wpleg_d = din("w_ple_gate", [1024, 1024])
    nppost_d = din("norm_ple_post", [1024])
    out_d = fw.dram("out", [NOWN, 1024], F32, kind="ExternalOutput")

    n1To_d = fw.dram("n1To", [1024, NOWN], BF16, kind=dk)
    kT_d = fw.dram("kT", [1024, NPOS], BF16, kind=dk)
    qT_d = fw.dram("qT", [1024, NOWN], BF16, kind=dk)
    v_d = fw.dram("vtm", [NPOS, 1024], BF16, kind=dk)
    xstm_d = fw.dram("xstm", [NPOS, 2048], BF16, kind=dk)
    btm_d = fw.dram("btm", [NPOS, 1024], BF16, kind=dk)
    bT_d = fw.dram("bT", [1024, NPOS], BF16, kind=dk)
    cT_d = fw.dram("cT", [1024, NPOS], BF16, kind=dk)
    zs_d = fw.dram("zs", [NOWN, 2048], F32, kind=dk)
    dtt_d = fw.dram("dtt", [128, NBLK, 32], F32, kind=dk)
    yssdT_d = fw.dram("yssdT", [2048, NOWN], BF16, kind=dk)
    ysbT_d = fw.dram("ysbT", [1024, NOWN], BF16, kind=dk)
    h1_d = fw.dram("h1", [NOWN, 1024], F32, kind=dk)
    n2T_d = fw.dram("n2T", [1024, NOWN], BF16, kind=dk)
    h2_d = fw.dram("h2", [NOWN, 1024], F32, kind=dk)
    h2T_d = fw.dram("h2T", [1024, NOWN], BF16, kind=dk)
    wg_b = fw.dram("wg_b", [1024, 2048], BF16)
    wssd_b = fw.dram("wssd_b", [2048, 1024], BF16)
    wsb_b = fw.dram("wsb_b", [1024, 1024], BF16)
    wout_b = fw.dram("wout_b", [1024, 1024], BF16)
    w1_b = fw.dram("w1_b", [1024, 4096], BF16)
    w2_b = fw.dram("w2_b", [4096, 1024], BF16)
    wpg_b = fw.dram("wpg_b", [1024, 1024], BF16)
    wp_b = fw.dram("wp_b", [256, 1024], BF16)

    banks = []
    ps_all = nc.alloc_psum_tensor("ps_all", [128, 8 * 512], F32)
    for i in range(8):
        h = ps_all[:, i * 512:(i + 1) * 512]
        banks.append(T(h, f"bank{i}", hb=h.bitcast(BF16)))
    bank_i = [0]

    def bank():
        b = banks[bank_i[0] % 8]
        bank_i[0] += 1
        return b

    with contextlib.ExitStack() as gs:
        ident = fw.sb(gs, [128, 128], BF16, "ident")
        tmpf = fw.sb(gs, [128, 128], F32, "tmpf")
        tri = fw.sb(gs, [128, 128], F32, "tri")
        onesf = fw.sb(gs, [128, 128], F32, "onesf")
        negm = fw.sb(gs, [128, 512], BF16, "negm")
        negm_ssd = fw.sb(gs, [128, 128], BF16, "negms")
        negU = fw.sb(gs, [128, 128], BF16, "negU")
        negO = fw.sb(gs, [128, 128], BF16, "negO")
        dtt = fw.sb(gs, [128, NBLK, 32], F32, "dtt")
        valid = fw.sb(gs, [128, NBLK], F32, "valid")

        def cmask(dst, val_keep, cmp, fill, base, cm, pat, n):
            op("pool", lambda: G.memset(tmpf[:, 0:n], val_keep), writes=[tmpf])
            op("pool", lambda: G.affine_select(out=tmpf[:, 0:n], in_=tmpf[:, 0:n], pattern=[[pat, n]],
                                                compare_op=cmp, fill=fill, base=base, channel_multiplier=cm),
               reads=[tmpf], writes=[tmpf])

        cmask(None, 1.0, ALU.is_equal, 0.0, 0, 1, -1, 128)
        op("dve", lambda: V.tensor_copy(out=ident[:], in_=tmpf[:]), reads=[tmpf], writes=[ident])
        cmask(None, 1.0, ALU.is_ge, 0.0, 0, -1, 1, 128)
        op("dve", lambda: V.tensor_copy(out=tri[:], in_=tmpf[:]), reads=[tmpf], writes=[tri])
        cmask(None, 0.0, ALU.is_ge, NEG, 0, -1, 1, 128)
        op("dve", lambda: V.tensor_copy(out=negm_ssd[:], in_=tmpf[:]), reads=[tmpf], writes=[negm_ssd])
        cmask(None, 0.0, ALU.is_gt, NEG, 0, -1, 1, 128)
        for j in range(4):
            op("dve", lambda: V.tensor_copy(out=negm[:, j * 128:(j + 1) * 128], in_=tmpf[:]), reads=[tmpf], writes=[negm])
        cmask(None, -1.0, ALU.is_ge, 0.0, 0, 1, -1, 128)
        op("dve", lambda: V.tensor_copy(out=negU[:], in_=tmpf[:]), reads=[tmpf], writes=[negU])
        op("pool", lambda: G.memset(negO[:], -1.0), writes=[negO])
        op("pool", lambda: G.memset(onesf[:], 1.0), writes=[onesf])
        dma("sp", valid[:], valid_d[:, :], valid, valid_d, valid)
        identf = fw.sb(gs, [128, 128], F32, "identf")
        cmask(None, 1.0, ALU.is_equal, 0.0, 0, 1, -1, 128)
        op("dve", lambda: V.tensor_copy(out=identf[:], in_=tmpf[:]), reads=[tmpf], writes=[identf])

        def load_fm(stk, dst, c0, src_ap, n, src_t):
            rows = fw.sb(stk, [n, 128], F32, "rows")
            dma("sp", rows[:], src_ap, rows, src_t, rows)
            bk = bank()
            op("pe", lambda: PE.transpose(out=bk[:, 0:n], in_=rows[:], identity=identf[0:n, 0:n]), reads=[rows, identf], writes=[bk])
            op("dve", lambda: V.tensor_copy(out=dst[:, c0:c0 + n], in_=bk[:, 0:n]), reads=[bk], writes=[dst])

        with contextlib.ExitStack() as st:
            n1T = fw.sb(st, [128, 8, NPOS], BF16, "n1T")
            n1c = [fw.view(n1T) for _ in range(8)]
            gfm = fw.sb(st, [128, 8], F32, "gfm")
            load_fm(st, gfm, 0, nmpre_d.h.rearrange("(k p) -> k p", p=128), 8, nmpre_d)
            wblk = [fw.sb(st, [128, 8, 512], BF16, "wblk") for _ in range(3)]
            stg = [fw.sb(st, [128, NPOS], BF16, "stg") for _ in range(3)]
            stA = contextlib.ExitStack()
            if True:
                xt = [fw.sb(stA, [128, 1024], F32, "xt") for _ in range(3)]
                junk = fw.sb(stA, [128, 1024], F32, "junk")
                ssq = [fw.sb(stA, [128, 1], F32, "ssq") for _ in range(3)]
                xn = [fw.sb(stA, [128, 1024], BF16, "xn") for _ in range(3)]
                pbanks = {}

                def P1(pb):
                    X, SS, XN = xt[pb % 3], ssq[pb % 3], xn[pb % 3]
                    dma("sp", X[:], xs_d[pb * 128:(pb + 1) * 128, :], X, xs_d, X)
                    op("dve", lambda: V.scalar_tensor_tensor(out=junk[:], in0=X[:], scalar=1.0, in1=X[:], op0=ALU.mult,
                                                             op1=ALU.mult, accum_out=SS[:]), reads=[X], writes=[SS])
                    op("act", lambda: S.activation(out=SS[:], in_=SS[:], func=AF.Sqrt, scale=1.0 / 1024, bias=EPS), reads=[SS], writes=[SS])
                    op("dve", lambda: V.reciprocal(out=SS[:], in_=SS[:]), reads=[SS], writes=[SS])
                    op("act", lambda: S.activation(out=XN[:], in_=X[:], func=AF.Identity, scale=SS[:]), reads=[X, SS], writes=[XN])
                    bk = bank()
                    pbanks[pb] = bk
                    pT = bk.hb.rearrange("p (k c) -> p k c", k=8)
                    for k in range(8):
                        op("pe", lambda: PE.transpose(out=pT[:, k, :], in_=XN[:, k * 128:(k + 1) * 128], identity=ident[:]), reads=[XN, ident], writes=[bk])

                def P2(pb):
                    bk = pbanks.pop(pb)
                    pT = bk.hb.rearrange("p (k c) -> p k c", k=8)
                    op("dve", lambda: V.tensor_tensor(out=n1T[:, :, pb * 128:(pb + 1) * 128], in0=pT, in1=bc_last(gfm[:], 128), op=ALU.mult),
                       reads=[bk, gfm], writes=[n1c[pb // 4]])

                P1(0)
                for pb in range(NBLK):
                    if pb + 1 < NBLK:
                        P1(pb + 1)
                    P2(pb)
            n1own = n1T.h.rearrange("p k (i two j) -> p k i two j", two=2, j=128)
            for c in n1c:
                fw._merge(n1T.w, c.w)
            for k in range(8):
                dma("sp", n1To_d.h.rearrange("(k p) (i j) -> p k i j", p=128, j=128)[:, k, :, :], n1own[:, k, :, 1, :], n1To_d, n1T, n1T)
            for c in n1c:
                fw._merge(c.r, n1T.r)

            wi = [0]
            win_v = win_d.h.rearrange("(k p) c -> p k c", p=128)

            def loadw(c0, n=512):
                w = wblk[wi[0] % 3]
                wi[0] += 1
                dma("pool", w[:, :, 0:n], win_v[:, :, c0:c0 + n], w, win_d, w)
                return w

            si = [0]

            def mm_fm(w, ft, rhs_of, nchunk, evac):
                for pc in range(nchunk):
                    bk = bank()
                    rd = rhs_of(pc)
                    for kc in range(8):
                        op("pe", lambda: PE.matmul(bk[:, :], lhsT=w[:, kc, ft * 128:(ft + 1) * 128], rhs=rd[0](kc),
                                                   start=(kc == 0), stop=(kc == 7)), reads=[w, rd[1]], writes=[bk])
                    evac(pc, bk)

            def rhs_all(pc):
                return (lambda kc: n1T[:, kc, pc * 512:(pc + 1) * 512], n1c[pc])

            def rhs_own(pc):
                return (lambda kc: n1own[:, kc, 4 * pc:4 * pc + 4, 1, :], n1c[2 * pc])

            for cb in range(2):
                w = loadw(OFF_K + cb * 512)
                for ft in range(4):
                    sg = stg[si[0] % 3]
                    si[0] += 1
                    mm_fm(w, ft, rhs_all, 8, lambda pc, bk: op("act", lambda: S.copy(out=sg[:, pc * 512:(pc + 1) * 512], in_=bk[:, :]), reads=[bk], writes=[sg]))
                    r0 = (cb * 4 + ft) * 128
                    dma("sp", kT_d[r0:r0 + 128, :], sg[:, :], kT_d, sg, sg)
            for cb in range(2):
                w = loadw(OFF_Q + cb * 512)
                for ft in range(4):
                    sg = stg[si[0] % 3]
                    si[0] += 1
                    for pc in range(4):
                        bk = bank()
                        for kc in range(8):
                            op("pe", lambda: PE.matmul(bk[:, :], lhsT=w[:, kc, ft * 128:(ft + 1) * 128], rhs=n1own[:, kc, 4 * pc:4 * pc + 4, 1, :],
                                                       start=(kc == 0), stop=(kc == 7)), reads=[w, n1c[2 * pc], n1c[2 * pc + 1]], writes=[bk])
                        op("act", lambda: S.activation(out=sg[:, pc * 512:(pc + 1) * 512], in_=bk[:, :], func=AF.Copy, scale=0.125), reads=[bk], writes=[sg])
                    r0 = (cb * 4 + ft) * 128
                    dma("sp", qT_d[r0:r0 + 128, :], sg[:, 0:NOWN], qT_d, sg, sg)
            stB1 = contextlib.ExitStack()
            wv = [loadw(OFF_V), loadw(OFF_V + 512)]
            vst = [fw.sb(stB1, [128, 1024], BF16, "vst") for _ in range(2)]
            for pb in range(NBLK):
                vs = vst[pb % 2]
                for cb in range(2):
                    bk = bank()
                    for kc in range(8):
                        op("pe", lambda: PE.matmul(bk[:, :], lhsT=n1T[:, kc, pb * 128:(pb + 1) * 128], rhs=wv[cb][:, kc, :],
                                                   start=(kc == 0), stop=(kc == 7)), reads=[wv[cb], n1c[pb // 4]], writes=[bk])
                    op("act", lambda: S.copy(out=vs[:, cb * 512:(cb + 1) * 512], in_=bk[:, :]), reads=[bk], writes=[vs])
                dma("sp", v_d[pb * 128:(pb + 1) * 128, :], vs[:, :], v_d, vs, vs)
            wdt = loadw(OFF_DT, 32)
            dtb = fw.sb(stB1, [128, 32], F32, "dtb")
            dma("sp", dtb[:], dtb_d.h.ap().partition_broadcast(128), dtb, dtb_d, dtb)
            dtmp = fw.sb(stB1, [128, 16, 32], F32, "dtmp")
            for half in range(2):
                bk = bank()
                bkv = bk.h.rearrange("p (a b) -> p a b", b=32)
                for j in range(16):
                    pb = half * 16 + j
                    for kc in range(8):
                        op("pe", lambda: PE.matmul(bkv[:, j, :], lhsT=n1T[:, kc, pb * 128:(pb + 1) * 128], rhs=wdt[:, kc, 0:32],
                                                   start=(kc == 0), stop=(kc == 7)), reads=[wdt, n1c[pb // 4]], writes=[bk])
                op("dve", lambda: V.tensor_tensor(out=dtmp[:], in0=bkv, in1=dtb[:].unsqueeze(1).to_broadcast([128, 16, 32]), op=ALU.add),
                   reads=[bk, dtb], writes=[dtmp])
                op("act", lambda: S.activation(out=dtmp[:], in_=dtmp[:], func=AF.Exp), reads=[dtmp], writes=[dtmp])
                op("act", lambda: S.activation(out=dtmp[:], in_=dtmp[:], func=AF.Ln, bias=1.0), reads=[dtmp], writes=[dtmp])
                op("dve", lambda: V.tensor_tensor(out=dtt[:, half * 16:(half + 1) * 16, :], in0=dtmp[:],
                                                  in1=bc_last(valid[:, half * 16:(half + 1) * 16], 32), op=ALU.mult),
                   reads=[dtmp, valid], writes=[dtt])
            if debug:
                dma("sp", dtt_d[:, :, :], dtt[:], dtt_d, dtt, dtt)
            zst = [fw.sb(stB1, [128, 512], F32, "zst") for _ in range(2)]
            zi = 0
            for cb in range(4):
                w = loadw(OFF_Z + cb * 512)
                for i in range(NOB):
                    pb = 2 * i + 1
                    bk = bank()
                    for kc in range(8):
                        op("pe", lambda: PE.matmul(bk[:, :], lhsT=n1T[:, kc, pb * 128:(pb + 1) * 128], rhs=w[:, kc, :],
                                                   start=(kc == 0), stop=(kc == 7)), reads=[w, n1c[pb // 4]], writes=[bk])
                    zt = zst[zi % 2]
                    zi += 1
                    op("act", lambda: S.activation(out=zt[:], in_=bk[:, :], func=AF.Silu), reads=[bk], writes=[zt])
                    dma("sp", zs_d[i * 128:(i + 1) * 128, cb * 512:(cb + 1) * 512], zt[:], zs_d, zt, zt)
            fw.barrier()
            stB1.close()
            stA.close()
            cw = fw.sb(st, [128, 4 * 32], F32, "cw")
            cbias = fw.sb(st, [128, 32], F32, "cbias")
            for k in range(4):
                load_fm(st, cw, k * 32, convw_d.h.rearrange("k (f p) -> k f p", p=128)[k], 32, convw_d)
            load_fm(st, cbias, 0, convb_d.h.rearrange("(f p) -> f p", p=128), 32, convb_d)
            pre = [fw.sb(st, [128, NPOS + 3], F32, "pre") for _ in range(2)]
            acc = [fw.sb(st, [128, NPOS], F32, "acc") for _ in range(2)]
            tm = [fw.sb(st, [128, NBLK, 128], BF16, "tm") for _ in range(2)]
            for p_ in pre:
                op("pool", lambda: G.memset(p_[:, 0:3], 0.0), writes=[p_])
            wx = {}
            sgs = {}

            def Mx(f):
                cb, ft = f // 4, f % 4
                if ft == 0:
                    wx[cb] = loadw(OFF_X + cb * 512)
                w, PRE = wx[cb], pre[f % 2]
                mm_fm(w, ft, rhs_all, 8, lambda pc, bk: op("act", lambda: S.copy(out=PRE[:, 3 + pc * 512:3 + (pc + 1) * 512], in_=bk[:, :]), reads=[bk], writes=[PRE]))

            def C1(f):
                PRE, ACC = pre[f % 2], acc[f % 2]
                op("act", lambda: S.activation(out=ACC[:], in_=PRE[:, 0:NPOS], func=AF.Identity, scale=cw[:, f:f + 1]), reads=[PRE, cw], writes=[ACC])
                for k in range(1, 4):
                    op("dve", lambda: V.scalar_tensor_tensor(out=ACC[:], in0=PRE[:, k:k + NPOS], scalar=cw[:, k * 32 + f:k * 32 + f + 1], in1=ACC[:],
                                                             op0=ALU.mult, op1=ALU.add), reads=[PRE, cw, ACC], writes=[ACC])

            def C2(f):
                ACC = acc[f % 2]
                sg = stg[si[0] % 3]
                si[0] += 1
                sgs[f] = sg
                op("act", lambda: S.activation(out=sg[:], in_=ACC[:], func=AF.Silu, bias=cbias[:, f:f + 1]), reads=[ACC, cbias], writes=[sg])
                if f >= 16:
                    dT = bT_d if f < 24 else cT_d
                    r0 = ((f - 16) % 8) * 128
                    dma("sp", dT[r0:r0 + 128, :], sg[:, :], dT, sg, sg)

            def Tx(f):
                sg = sgs.pop(f)
                if f >= 24:
                    return
                tmt = tm[f % 2]
                for grp in range(4):
                    bk = bank()
                    pT = bk.hb.rearrange("p (k c) -> p k c", k=8)
                    for j in range(8):
                        pb = grp * 8 + j
                        op("pe", lambda: PE.transpose(out=pT[:, j, :], in_=sg[:, pb * 128:(pb + 1) * 128], identity=ident[:]), reads=[sg, ident], writes=[bk])
                    op("act", lambda: S.copy(out=tmt[:, grp * 8:(grp + 1) * 8, :], in_=pT), reads=[bk], writes=[tmt])
                dd, c0 = (xstm_d, f * 128) if f < 16 else (btm_d, (f - 16) * 128)
                dma("sp", dd.h.rearrange("(b p) c -> p b c", p=128)[:, :, c0:c0 + 128], tmt[:], dd, tmt, tmt)

            Mx(0)
            for f in range(32 + 2):
                if f + 1 < 32:
                    Mx(f + 1)
                if f < 32:
                    C1(f)
                if 0 <= f - 1 < 32:
                    C2(f - 1)
                if 0 <= f - 2 < 32:
                    Tx(f - 2)
            fw.barrier()
        print("phaseAB ninst", fw.ninst, "nwaits", fw.nwaits)
        if debug == 1:
            dma("sp", out_d[0:128, :], xs_d[0:128, :], out_d, xs_d, out_d)
            fw.barrier()
            return nc

        with contextlib.ExitStack() as st:
            a_rep = fw.sb(st, [128, 32], F32, "a_rep")
            dsk = fw.sb(st, [128, 32], F32, "dsk")
            ssdn = fw.sb(st, [128, 2048], F32, "ssdn")
            dma("sp", a_rep[:], alog_d.h.ap().partition_broadcast(128), a_rep, alog_d, a_rep)
            dma("sp", dsk[:], dskip_d.h.ap().partition_broadcast(128), dsk, dskip_d, dsk)
            dma("sp", ssdn[:], ssdn_d.h.ap().partition_broadcast(128), ssdn, ssdn_d, ssdn)
            op("act", lambda: S.activation(out=a_rep[:], in_=a_rep[:], func=AF.Exp), reads=[a_rep], writes=[a_rep])
            op("dve", lambda: V.tensor_scalar(out=a_rep[:], in0=a_rep[:], scalar1=-1.0, scalar2=None, op0=ALU.mult), reads=[a_rep], writes=[a_rep])
            S32 = fw.sb(st, [128, 8, 256], F32, "S32")
            Sbf = [fw.sb(st, [128, 8, 256], BF16, "Sbf") for _ in range(2)]
            op("pool", lambda: G.memset(S32[:], 0.0), writes=[S32])
            op("pool", lambda: G.memset(Sbf[0][:], 0.0), writes=[Sbf[0]])
            S32v = [fw.view(S32) for _ in range(4)]
            Sbfv = [[fw.view(Sbf[k]) for _ in range(4)] for k in range(2)]
            for v_ in S32v:
                fw._merge(v_.w, S32.w)
            for v_ in Sbfv[0]:
                fw._merge(v_.w, Sbf[0].w)
            xs_c = [fw.sb(st, [128, 2048], BF16, "xs_c") for _ in range(3)]
            btm_c = [fw.sb(st, [128, 1024], BF16, "btm_c") for _ in range(3)]
            bT_cc = [fw.sb(st, [128, 8, 128], BF16, "bT_c") for _ in range(2)]
            cT_cc = [fw.sb(st, [128, 8, 128], BF16, "cT_c") for _ in range(2)]
            zs_cc = [fw.sb(st, [128, 2048], F32, "zs_c") for _ in range(2)]
            dA = [fw.sb(st, [128, 32], F32, "dA") for _ in range(2)]
            acs = [fw.sb(st, [128, 32], F32, "acs") for _ in range(2)]
            negacs = fw.sb(st, [128, 32], F32, "negacs")
            cd = [fw.sb(st, [128, 32], F32, "cd") for _ in range(2)]
            dte = [fw.sb(st, [128, 32], F32, "dte") for _ in range(2)]
            w1 = [fw.sb(st, [128, 32], F32, "w1") for _ in range(2)]
            Eac = fw.sb(st, [128, 32], F32, "Eac")
            xdd = [fw.sb(st, [128, 2048], BF16, "xdd") for _ in range(2)]
            xd = fw.sb(st, [128, 2048], BF16, "xd")
            dec = [fw.sb(st, [128, 128], F32, "dec") for _ in range(8)]
            GT = [fw.sb(st, [128, 128], BF16, "GT") for _ in range(8)]
            ysb2 = [fw.sb(st, [128, 2048], F32, "ysb") for _ in range(2)]
            ytmp = fw.sb(st, [128, 2048], F32, "ytmp")
            t256 = [fw.sb(st, [128, 256], F32, "t256") for _ in range(2)]
            ssg = fw.sb(st, [128, 8], F32, "ssg")
            ynb = [fw.sb(st, [128, 2048], BF16, "ynb") for _ in range(2)]
            yT = [fw.sb(st, [128, 16, 128], BF16, "yT") for _ in range(2)]
            junk2 = fw.sb(st, [128, 256], F32, "junk2")
            bTv = bT_d.h.rearrange("(g n) t -> n g t", n=128)
            cTv = cT_d.h.rearrange("(g n) t -> n g t", n=128)
            yssdTv = yssdT_d.h.rearrange("(f p) t -> p f t", p=128)

            def loadC(c):
                if c >= NBLK:
                    return
                XS, BTM = xs_c[c % 3], btm_c[c % 3]
                dma("sp", XS[:], xstm_d[c * 128:(c + 1) * 128, :], XS, xstm_d, XS)
                dma("sp", BTM[:], btm_d[c * 128:(c + 1) * 128, :], BTM, btm_d, BTM)
                if c % 2 == 1:
                    i = c // 2
                    dma("sp", bT_cc[i % 2][:], bTv[:, :, c * 128:(c + 1) * 128], bT_cc[i % 2], bT_d, bT_cc[i % 2])
                    dma("sp", cT_cc[i % 2][:], cTv[:, :, c * 128:(c + 1) * 128], cT_cc[i % 2], cT_d, cT_cc[i % 2])
                    dma("sp", zs_cc[i % 2][:], zs_d[i * 128:(i + 1) * 128, :], zs_cc[i % 2], zs_d, zs_cc[i % 2])

            def flushT(i):
                YN, YT = ynb[i % 2], yT[i % 2]
                for half in range(2):
                    bk = bank()
                    pT = bk.hb.rearrange("p (k c) -> p k c", k=8)
                    for j in range(8):
                        f = half * 8 + j
                        op("pe", lambda: PE.transpose(out=pT[:, j, :], in_=YN[:, f * 128:(f + 1) * 128], identity=ident[:]), reads=[YN, ident], writes=[bk])
                    op("act", lambda: S.copy(out=YT[:, half * 8:(half + 1) * 8, :], in_=pT), reads=[bk], writes=[YT])
                dma("sp", yssdTv[:, :, i * 128:(i + 1) * 128], YT[:], yssdT_d, YT, YT)

            hi = 0
            pending = []
            ytmp2 = [ytmp, fw.sb(st, [128, 2048], F32, "ytmp")]
            loadC(0)
            loadC(1)

            def stageA(c):
                own = (c % 2 == 1)
                i = c // 2
                XS, DA, ACS, CD, DTE, W1, XDD = xs_c[c % 3], dA[c % 2], acs[c % 2], cd[c % 2], dte[c % 2], w1[c % 2], xdd[c % 2]
                XS3 = XS.h.rearrange("p (h d) -> p h d", d=64)
                if own:
                    op("pool", lambda: G.tensor_tensor(out=ytmp2[i % 2].h.rearrange("p (h d) -> p h d", d=64), in0=XS3, in1=bc_last(dsk[:], 64), op=ALU.mult), reads=[XS, dsk], writes=[ytmp2[i % 2]])
                op("dve", lambda: V.tensor_tensor(out=DA[:], in0=dtt[:, c, :], in1=a_rep[:], op=ALU.mult), reads=[dtt, a_rep], writes=[DA])
                bA = bank()
                op("pe", lambda: PE.matmul(bA[:, 0:32], lhsT=tri[:], rhs=DA[:], start=True, stop=True), reads=[tri, DA], writes=[bA])
                op("pe", lambda: PE.matmul(bA[:, 32:64], lhsT=onesf[:], rhs=DA[:], start=True, stop=True), reads=[onesf, DA], writes=[bA])
                op("dve", lambda: V.tensor_copy(out=ACS[:], in_=bA[:, 0:32]), reads=[bA], writes=[ACS])
                op("dve", lambda: V.tensor_tensor(out=DTE[:], in0=bA[:, 32:64], in1=ACS[:], op=ALU.subtract), reads=[bA, ACS], writes=[DTE])
                op("act", lambda: S.activation(out=DTE[:], in_=DTE[:], func=AF.Exp), reads=[DTE], writes=[DTE])
                op("act", lambda: S.activation(out=CD[:], in_=bA[:, 32:64], func=AF.Exp), reads=[bA], writes=[CD])
                op("dve", lambda: V.tensor_tensor(out=W1[:], in0=dtt[:, c, :], in1=DTE[:], op=ALU.mult), reads=[dtt, DTE], writes=[W1])
                op("pool", lambda: G.tensor_tensor(out=XDD.h.rearrange("p (h d) -> p h d", d=64), in0=XS3, in1=bc_last(W1[:], 64), op=ALU.mult),
                   reads=[XS, W1], writes=[XDD])
                if own:
                    op("dve", lambda: V.tensor_tensor(out=xd.h.rearrange("p (h d) -> p h d", d=64), in0=XS3, in1=bc_last(dtt[:, c, :], 64), op=ALU.mult),
                       reads=[XS, dtt], writes=[xd])
                    op("dve", lambda: V.tensor_scalar(out=negacs[:], in0=ACS[:], scalar1=-1.0, scalar2=None, op0=ALU.mult), reads=[ACS], writes=[negacs])
                    op("act", lambda: S.activation(out=Eac[:], in_=ACS[:], func=AF.Exp), reads=[ACS], writes=[Eac])

            stageA(0)
            for c in range(NBLK):
                own = (c % 2 == 1)
                i = c // 2
                loadC(c + 2)
                if c + 1 < NBLK:
                    stageA(c + 1)
                BTM, DA, CD, XDD = btm_c[c % 3], dA[c % 2], cd[c % 2], xdd[c % 2]
                bT_c, cT_c, zs_c = bT_cc[i % 2], cT_cc[i % 2], zs_cc[i % 2]
                ysb = ysb2[i % 2]
                ytmp = ytmp2[i % 2]
                Sprev, Snext = Sbf[c % 2], Sbf[(c + 1) % 2]
                if own:
                    gb = {}

                    def G1(g):
                        X1, X2, X3 = bank(), bank(), bank()
                        gb[g] = (X1, X2, X3)
                        op("pe", lambda: PE.matmul(X1[:, 0:128], lhsT=bT_c[:, g, :], rhs=cT_c[:, g, :], start=True, stop=True), reads=[bT_c, cT_c], writes=[X1])
                        for hh in range(4):
                            h = 4 * g + hh
                            o = X2[:, hh * 128:(hh + 1) * 128]
                            op("pe", lambda: PE.matmul(o, lhsT=DA[:, h:h + 1].to_broadcast([128, 128]), rhs=tri[:], start=True, stop=False), reads=[DA, tri], writes=[X2])
                            op("pe", lambda: PE.matmul(o, lhsT=ident[:], rhs=negm_ssd[:], start=False, stop=True), reads=[ident, negm_ssd], writes=[X2])

                    def G2(g):
                        X1, X2, X3 = gb[g]
                        for hh in range(4):
                            h = 4 * g + hh
                            DEC, GTT = dec[(g % 2) * 4 + hh], GT[(g % 2) * 4 + hh]
                            op("act", lambda: S.activation(out=DEC[:], in_=X2[:, hh * 128:(hh + 1) * 128], func=AF.Exp, bias=negacs[:, h:h + 1]), reads=[X2, negacs], writes=[DEC])
                        for hh in range(4):
                            DEC, GTT = dec[(g % 2) * 4 + hh], GT[(g % 2) * 4 + hh]
                            op("dve", lambda: V.tensor_tensor(out=GTT[:], in0=DEC[:], in1=X1[:, 0:128], op=ALU.mult), reads=[DEC, X1], writes=[GTT])

                    def G3(g):
                        X1, X2, X3 = gb.pop(g)
                        op("pe", lambda: PE.matmul(X3[:, 0:256], lhsT=cT_c[:, g, :], rhs=Sprev[:, g, :], start=True, stop=True), reads=[cT_c, Sbfv[c % 2][g // 2]], writes=[X3])
                        for hh in range(4):
                            h = 4 * g + hh
                            GTT = GT[(g % 2) * 4 + hh]
                            op("pe", lambda: PE.matmul(X1[:, 128 + hh * 64:128 + (hh + 1) * 64], lhsT=GTT[:], rhs=xd[:, h * 64:(h + 1) * 64], start=True, stop=True),
                               reads=[GTT, xd], writes=[X1])
                        T2 = t256[g % 2]
                        op("dve", lambda: V.tensor_tensor(out=T2.h.rearrange("p (h d) -> p h d", d=64), in0=X3.h.rearrange("p (h d) -> p h d", d=64)[:, 0:4, :],
                                                          in1=bc_last(Eac[:, 4 * g:4 * g + 4], 64), op=ALU.mult), reads=[X3, Eac], writes=[T2])
                        op("dve", lambda: V.tensor_tensor(out=ysb[:, g * 256:(g + 1) * 256], in0=T2[:], in1=X1[:, 128:384], op=ALU.add), reads=[T2, X1], writes=[ysb])

                    G1(0)
                    for g in range(8):
                        if g >= 1:
                            G3(g - 1)
                        if g + 1 < 8:
                            G1(g + 1)
                        G2(g)
                    G3(7)
                    if i >= 1:
                        flushT(i - 1)
                    def post(ysb=ysb, zs_c=zs_c, i=i, ytmp=ytmp):
                        op("dve", lambda: V.tensor_tensor(out=ysb[:], in0=ysb[:], in1=ytmp[:], op=ALU.add), reads=[ysb, ytmp], writes=[ysb])
                        op("dve", lambda: V.tensor_tensor(out=ysb[:], in0=ysb[:], in1=zs_c[:], op=ALU.mult), reads=[ysb, zs_c], writes=[ysb])
                        for g in range(8):
                            op("act", lambda: S.activation(out=junk2[:], in_=ysb[:, g * 256:(g + 1) * 256], func=AF.Square, accum_out=ssg[:, g:g + 1]), reads=[ysb], writes=[ssg])
                        op("act", lambda: S.activation(out=ssg[:], in_=ssg[:], func=AF.Sqrt, scale=1.0 / 256, bias=EPS), reads=[ssg], writes=[ssg])
                        op("dve", lambda: V.reciprocal(out=ssg[:], in_=ssg[:]), reads=[ssg], writes=[ssg])
                        for g in range(8):
                            op("act", lambda: S.activation(out=ysb[:, g * 256:(g + 1) * 256], in_=ysb[:, g * 256:(g + 1) * 256], func=AF.Identity, scale=ssg[:, g:g + 1]),
                               reads=[ysb, ssg], writes=[ysb])
                        op("dve", lambda: V.tensor_tensor(out=ynb[i % 2][:], in0=ysb[:], in1=ssdn[:], op=ALU.mult), reads=[ysb, ssdn], writes=[ynb[i % 2]])

                    pending.append(post)
                if c < NBLK - 1:
                    for gp in range(4):
                        bS = bank()
                        for gg in range(2):
                            g = gp * 2 + gg
                            op("pe", lambda: PE.matmul(bS[:, gg * 256:(gg + 1) * 256], lhsT=BTM[:, g * 128:(g + 1) * 128], rhs=XDD[:, g * 256:(g + 1) * 256],
                                                       start=True, stop=True), reads=[BTM, XDD], writes=[bS])
                        Sv = S32.h.rearrange("p g (h d) -> p (g h) d", d=64)[:, gp * 8:(gp + 1) * 8, :]
                        op("dve", lambda: V.tensor_tensor(out=Sv, in0=Sv, in1=bc_last(CD[:, gp * 8:(gp + 1) * 8], 64), op=ALU.mult), reads=[S32v[gp], CD], writes=[S32v[gp]])
                        op("dve", lambda: V.tensor_tensor(out=Sv, in0=Sv, in1=bS.h.rearrange("p (h d) -> p h d", d=64), op=ALU.add), reads=[S32v[gp], bS], writes=[S32v[gp]])
                        op("act", lambda: S.copy(out=Snext[:, 2 * gp:2 * gp + 2, :], in_=S32[:, 2 * gp:2 * gp + 2, :]), reads=[S32v[gp]], writes=[Sbfv[(c + 1) % 2][gp]])
                while len(pending) > (1 if own else 0):
                    pending.pop(0)()
            while pending:
                pending.pop(0)()
            flushT(NOB - 1)
            fw.barrier()
        print("phaseC ninst", fw.ninst, "nwaits", fw.nwaits)
        if debug == 2:
            dma("sp", out_d[0:128, :], xs_d[0:128, :], out_d, xs_d, out_d)
            fw.barrier()
            return nc

        stDE = contextlib.ExitStack()
        WE12 = (fw.sb(stDE, [128, 8, 2048], BF16, "Wg"), fw.sb(stDE, [128, 16, 1024], BF16, "Wssd"),
                fw.sb(stDE, [128, 8, 1024], BF16, "Wsb"), fw.sb(stDE, [128, 8, 1024], BF16, "Wout"))
        for w_, src_, nk_ in zip(WE12, (wgate_d, wssd_d, wsb_d, wout_d), (8, 16, 8, 8)):
            sv_ = src_.h.rearrange("(k p) c -> p k c", p=128)
            for kc0 in range(0, nk_, 2):
                dma("pool", w_[:, kc0:kc0 + 2, :], sv_[:, kc0:kc0 + 2, :], w_, src_, w_)
        for src_, dst_ in ((wff1_d, w1_b), (wff2_d, w2_b), (wpleg_d, wpg_b), (wple_d, wp_b)):
            dst_.bg = True
            nr = src_.h.shape[0]
            step = 256 if nr >= 256 else nr
            for r0 in range(0, nr, step):
                dma("pool", dst_[r0:r0 + step, :], src_[r0:r0 + step, :], dst_, src_, dst_)
        with contextlib.ExitStack() as st:
            KT = [fw.sb(st, [128, 2, NPOS], BF16, "KT")] * 2
            QZ = [fw.sb(st, [128, 2, NOB, 2, 128], BF16, "QZ")] * 2
            op("dve", lambda: V.memset(QZ[0][:], 0.0), writes=[QZ[0]])
            Vt = [fw.sb(st, [128, NBLK, 256], BF16, "Vt")] * 2
            e_p = [fw.sb(st, [128, 1024], F32, "e_p") for _ in range(2)]
            L_p = [fw.sb(st, [128, 1024], BF16, "L_p") for _ in range(4)]
            w_p = [fw.sb(st, [128, 1024], BF16, "w_p") for _ in range(3)]
            Suf32 = [fw.sb(st, [128, 512], F32, "Suf32") for _ in range(2)]
            SufB = [fw.sb(st, [128, 512], BF16, "SufB") for _ in range(2)]
            yst = [fw.sb(st, [128, 2, NOWN], BF16, "yst") for _ in range(2)]
            kTv = kT_d.h.rearrange("(f p) t -> p f t", p=128)
            qTv = qT_d.h.rearrange("(f p) t -> p f t", p=128)
            vv = v_d.h.rearrange("(b p) c -> p b c", p=128)
            ysbTv = ysbT_d.h.rearrange("(f p) t -> p f t", p=128)
            units = []
            for bq in range(4):
                for i in range(NOB):
                    for u, kb in enumerate(range(2 * i + 1, -1, -1)):
                        units.append((bq, i, kb, u == 0, kb == 0))
            NU = len(units)
            NP = NU // 2
            OB = [banks[0], banks[1]]

            def Xb(n):
                return banks[2 + 2 * ((n // 2) % 3) + n % 2]

            def Xpair(pp):
                k = 2 + 2 * (pp % 3)
                return ps_all[:, k * 512:(k + 2) * 512]

            def Lu(n):
                return L_p[(n // 2) % 4][:, (n % 2) * 512:(n % 2 + 1) * 512]

            def load_KQ(bq):
                K_, Q_ = KT[0], QZ[0]
                dma("sp", K_[:], kTv[:, 2 * bq:2 * bq + 2, :], K_, kT_d, K_)
                for pr in range(2):
                    for hb in range(2):
                        r0 = (2 * bq + pr) * 128 + hb * 64
                        dma("sp", Q_[hb * 64:(hb + 1) * 64, pr, :, hb, :], qT_d.h.rearrange("r (i t) -> r i t", t=128)[r0:r0 + 64], Q_, qT_d, Q_)

            def load_V(bq):
                dma("sp", Vt[0][:], vv[:, :, bq * 256:(bq + 1) * 256], Vt[0], v_d, Vt[0])

            def S1(pp):
                EP, LP = e_p[pp % 2], L_p[pp % 4]
                for n in (2 * pp, 2 * pp + 1):
                    bq, i, kb, first, last = units[n]
                    if i == 0 and first:
                        load_KQ(bq)
                    K_, Q_, X = KT[0], QZ[0], Xb(n)
                    for pr in range(2):
                        op("pe", lambda: PE.matmul(X[:, pr * 256:(pr + 1) * 256], lhsT=K_[:, pr, kb * 128:(kb + 1) * 128],
                                                   rhs=Q_[:, pr, i, :, :], start=(pr == 0), stop=(pr == 1 and not first)), reads=[K_, Q_], writes=[X])
                    if first:
                        op("pe", lambda: PE.matmul(X[:, :], lhsT=ident[:], rhs=negm[:], start=False, stop=True), reads=[ident, negm], writes=[X])
                xs_ = [Xb(2 * pp), Xb(2 * pp + 1)]
                op("act", lambda: S.activation(out=EP[:], in_=Xpair(pp), func=AF.Exp), reads=xs_, writes=[EP])
                op("act", lambda: S.activation(out=LP[:], in_=EP[:], func=AF.Ln, bias=1.0), reads=[EP], writes=[LP])

            def S2(n):
                bq, i, kb, first, last = units[n]
                if last:
                    return
                LP, SF, SBn = L_p[(n // 2) % 4], Suf32[i % 2], SufB[(n + 1) % 2]
                if first:
                    op("dve", lambda: V.tensor_copy(out=SF[:], in_=Lu(n)), reads=[LP], writes=[SF])
                    op("dve", lambda: V.tensor_copy(out=SBn[:], in_=Lu(n)), reads=[LP], writes=[SBn])
                else:
                    op("dve", lambda: V.tensor_tensor(out=SF[:], in0=SF[:], in1=Lu(n), op=ALU.add), reads=[SF, LP], writes=[SF])
                    op("dve", lambda: V.tensor_copy(out=SBn[:], in_=SF[:]), reads=[SF], writes=[SBn])

            def S3a_mm(n):
                bq, i, kb, first, last = units[n]
                X, LP, SBp = Xb(n), L_p[(n // 2) % 4], SufB[n % 2]
                op("pe", lambda: PE.matmul(X[:, :], lhsT=negU[:], rhs=Lu(n), start=False, stop=first, skip_group_check=True), reads=[negU, LP], writes=[X])
                if not first:
                    op("pe", lambda: PE.matmul(X[:, :], lhsT=negO[:], rhs=SBp[:], start=False, stop=True, skip_group_check=True), reads=[negO, SBp], writes=[X])

            def S3a_act(pp):
                WP = w_p[pp % 3]
                op("act", lambda: S.activation(out=WP[:], in_=Xpair(pp), func=AF.Exp), reads=[Xb(2 * pp), Xb(2 * pp + 1)], writes=[WP])

            def S3b(n):
                bq, i, kb, first, last = units[n]
                if i == 0 and first:
                    load_V(bq)
                V_, YS, WP = Vt[0], yst[bq % 2], w_p[(n // 2) % 3]
                c0 = (n % 2) * 512
                for j in range(4):
                    pr, hb = j // 2, j % 2
                    op("pe", lambda: PE.matmul(OB[pr][hb * 64:(hb + 1) * 64, 0:128], lhsT=V_[:, kb, j * 64:(j + 1) * 64], rhs=WP[:, c0 + j * 128:c0 + (j + 1) * 128],
                                               start=first, stop=last), reads=[V_, WP], writes=[OB[pr]])
                if last:
                    for pr in range(2):
                        op("dve", lambda: V.tensor_copy(out=YS[:, pr, i * 128:(i + 1) * 128], in_=OB[pr][:, 0:128]), reads=[OB[pr]], writes=[YS])
                    if i == NOB - 1:
                        dma("sp", ysbTv[:, 2 * bq:2 * bq + 2, :], YS[:], ysbT_d, YS, YS)

            S1(0)
            S1(1)
            for pp in range(NP):
                S2(2 * pp)
                S3a_mm(2 * pp)
                S3a_mm(2 * pp + 1)
                S3a_act(pp)
                if pp + 2 < NP:
                    S1(pp + 2)
                S2(2 * pp + 1)
                if pp >= 1:
                    S3b(2 * pp - 2)
                    S3b(2 * pp - 1)
            S3b(NU - 2)
            S3b(NU - 1)
            fw.barrier()
        print("phaseD ninst", fw.ninst, "nwaits", fw.nwaits)
        if debug == 3:
            dma("sp", out_d[0:128, :], xs_d[0:128, :], out_d, xs_d, out_d)
            fw.barrier()
            stDE.close()
            return nc

        def rep_load(stk, src_d, n, name):
            t = fw.sb(stk, [128, n], F32, name)
            dma("sp", t[:], src_d.h.ap().partition_broadcast(128), t, src_d, t)
            return t

        def norm_residual(ss2, junkE, bks, rep, resid, dst):
            for half in range(2):
                op("act", lambda: S.activation(out=junkE[:, 0:512], in_=bks[half][:, :], func=AF.Square, accum_out=ss2[:, half:half + 1]), reads=[bks[half]], writes=[ss2])
            op("dve", lambda: V.tensor_tensor(out=ss2[:, 2:3], in0=ss2[:, 0:1], in1=ss2[:, 1:2], op=ALU.add), reads=[ss2], writes=[ss2])
            op("act", lambda: S.activation(out=ss2[:, 2:3], in_=ss2[:, 2:3], func=AF.Sqrt, scale=1.0 / 1024, bias=EPS), reads=[ss2], writes=[ss2])
            op("dve", lambda: V.reciprocal(out=ss2[:, 2:3], in_=ss2[:, 2:3]), reads=[ss2], writes=[ss2])
            for half in range(2):
                op("dve", lambda: V.scalar_tensor_tensor(out=dst[:, half * 512:(half + 1) * 512], in0=bks[half][:, :], scalar=ss2[:, 2:3],
                                                         in1=rep[:, half * 512:(half + 1) * 512], op0=ALU.mult, op1=ALU.mult), reads=[bks[half], ss2, rep], writes=[dst])
            op("dve", lambda: V.tensor_tensor(out=dst[:], in0=dst[:], in1=resid[:], op=ALU.add), reads=[dst, resid], writes=[dst])

        def transposes_to(src16, bk, dst_fn):
            pT = bk.hb.rearrange("p (k c) -> p k c", k=8)
            for k in range(8):
                op("pe", lambda: PE.transpose(out=pT[:, k, :], in_=src16[:, k * 128:(k + 1) * 128], identity=ident[:]), reads=[src16, ident], writes=[bk])
            dst_fn(pT)

        with contextlib.ExitStack() as st:
            Wg, Wssd, Wsb, Wout = WE12
            bg = fw.sb(st, [128, 16], F32, "bg")
            load_fm(st, bg, 0, bgate_d.h.rearrange("(f p) -> f p", p=128), 16, bgate_d)
            gfm2 = fw.sb(st, [128, 8], F32, "gfm2")
            load_fm(st, gfm2, 0, nfpre_d.h.rearrange("(k p) -> k p", p=128), 8, nfpre_d)
            nmpost = rep_load(st, nmpost_d, 1024, "nmpost")
            n1cE = [fw.sb(st, [128, 8, 512], BF16, "n1cE")] * 2
            yssE = [fw.sb(st, [128, 16, 512], BF16, "yssE")] * 2
            ysbE = [fw.sb(st, [128, 8, 512], BF16, "ysbE") for _ in range(2)]
            mTt = [fw.sb(st, [128, 8, 512], BF16, "mT")] * 2
            gsT = [fw.sb(st, [128, 512], F32, "gsT") for _ in range(2)]
            gbT = [fw.sb(st, [128, 512], F32, "gbT") for _ in range(2)]
            m1T = [fw.sb(st, [128, 512], F32, "m1T") for _ in range(2)]
            m2T = [fw.sb(st, [128, 512], F32, "m2T") for _ in range(2)]
            xb = [fw.sb(st, [128, 1024], F32, "xb") for _ in range(2)]
            h1t = [fw.sb(st, [128, 1024], F32, "h1t") for _ in range(2)]
            xn2 = [fw.sb(st, [128, 1024], BF16, "xn2") for _ in range(2)]
            n2st = [fw.sb(st, [128, 8, 128], BF16, "n2st") for _ in range(2)]
            ss2 = [fw.sb(st, [128, 4], F32, "ss2") for _ in range(2)]
            junkE = fw.sb(st, [128, 1024], F32, "junkE")
            n1Tov = n1To_d.h.rearrange("(k p) t -> p k t", p=128)
            yssv = yssdT_d.h.rearrange("(k p) t -> p k t", p=128)
            ysbv = ysbT_d.h.rearrange("(k p) t -> p k t", p=128)
            n2Tv = n2T_d.h.rearrange("(k p) t -> p k t", p=128)
            pend = []

            def loadE(tc):
                if tc >= 4:
                    return
                sl = slice(tc * 512, (tc + 1) * 512)
                dma("sp", n1cE[tc % 2][:], n1Tov[:, :, sl], n1cE[tc % 2], n1To_d, n1cE[tc % 2])
                dma("sp", yssE[tc % 2][:], yssv[:, :, sl], yssE[tc % 2], yssdT_d, yssE[tc % 2])
                dma("sp", ysbE[tc % 2][:], ysbv[:, :, sl], ysbE[tc % 2], ysbT_d, ysbE[tc % 2])

            def Bm(tc, tb):
                blk = 4 * tc + tb
                mT = mTt[tc % 2]
                bks = [banks[(blk % 2) * 2], banks[(blk % 2) * 2 + 1]]
                dma("sp", xb[blk % 2][:], xs_d[(2 * blk + 1) * 128:(2 * blk + 2) * 128, :], xb[blk % 2], xs_d, xb[blk % 2])
                for half in range(2):
                    for kc in range(8):
                        op("pe", lambda: PE.matmul(bks[half][:, :], lhsT=mT[:, kc, tb * 128:(tb + 1) * 128], rhs=Wout[:, kc, half * 512:(half + 1) * 512],
                                                   start=(kc == 0), stop=(kc == 7)), reads=[mT, Wout], writes=[bks[half]])

            def Bn(tc, tb):
                blk = 4 * tc + tb
                XB, H1, XN2, SS = xb[blk % 2], h1t[blk % 2], xn2[blk % 2], ss2[blk % 2]
                bks = [banks[(blk % 2) * 2], banks[(blk % 2) * 2 + 1]]
                norm_residual(SS, junkE, bks, nmpost, XB, H1)
                dma("sp", h1_d[blk * 128:(blk + 1) * 128, :], H1[:], h1_d, H1, H1)
                op("dve", lambda: V.scalar_tensor_tensor(out=junkE[:], in0=H1[:], scalar=1.0, in1=H1[:], op0=ALU.mult, op1=ALU.mult, accum_out=SS[:, 3:4]), reads=[H1], writes=[SS])
                op("act", lambda: S.activation(out=SS[:, 3:4], in_=SS[:, 3:4], func=AF.Sqrt, scale=1.0 / 1024, bias=EPS), reads=[SS], writes=[SS])
                op("dve", lambda: V.reciprocal(out=SS[:, 3:4], in_=SS[:, 3:4]), reads=[SS], writes=[SS])
                op("act", lambda: S.activation(out=XN2[:], in_=H1[:], func=AF.Identity, scale=SS[:, 3:4]), reads=[H1, SS], writes=[XN2])

            def Bt(tc, tb):
                blk = 4 * tc + tb
                XN2, N2S = xn2[blk % 2], n2st[blk % 2]
                bk = banks[(blk % 2) * 2]
                transposes_to(XN2, bk, lambda pT: op("dve", lambda: V.tensor_tensor(out=N2S[:], in0=pT, in1=bc_last(gfm2[:], 128), op=ALU.mult), reads=[bk, gfm2], writes=[N2S]))
                dma("sp", n2Tv[:, :, blk * 128:(blk + 1) * 128], N2S[:], n2T_d, N2S, N2S)

            for tc in range(4):
                loadE(tc)
                N1, YSS, YSB, mT = n1cE[tc % 2], yssE[tc % 2], ysbE[tc % 2], mTt[tc % 2]
                for of in range(8):
                    GS, GB, M1, M2 = gsT[of % 2], gbT[of % 2], m1T[of % 2], m2T[of % 2]
                    b1, b2, b3, b4 = banks[4], banks[5], banks[6], banks[7]
                    for kc in range(8):
                        op("pe", lambda: PE.matmul(b1[:, :], lhsT=Wg[:, kc, of * 128:(of + 1) * 128], rhs=N1[:, kc, :], start=(kc == 0), stop=(kc == 7)), reads=[Wg, N1], writes=[b1])
                    op("act", lambda: S.activation(out=GS[:], in_=b1[:, :], func=AF.Sigmoid, bias=bg[:, of:of + 1]), reads=[b1, bg], writes=[GS])
                    for kc in range(8):
                        op("pe", lambda: PE.matmul(b2[:, :], lhsT=Wg[:, kc, 1024 + of * 128:1024 + (of + 1) * 128], rhs=N1[:, kc, :], start=(kc == 0), stop=(kc == 7)), reads=[Wg, N1], writes=[b2])
                    op("act", lambda: S.activation(out=GB[:], in_=b2[:, :], func=AF.Sigmoid, bias=bg[:, 8 + of:9 + of]), reads=[b2, bg], writes=[GB])
                    for kc in range(16):
                        op("pe", lambda: PE.matmul(b3[:, :], lhsT=Wssd[:, kc, of * 128:(of + 1) * 128], rhs=YSS[:, kc, :], start=(kc == 0), stop=(kc == 15)), reads=[Wssd, YSS], writes=[b3])
                    op("dve", lambda: V.tensor_tensor(out=M1[:], in0=GS[:], in1=b3[:, :], op=ALU.mult), reads=[GS, b3], writes=[M1])
                    for kc in range(8):
                        op("pe", lambda: PE.matmul(b4[:, :], lhsT=Wsb[:, kc, of * 128:(of + 1) * 128], rhs=YSB[:, kc, :], start=(kc == 0), stop=(kc == 7)), reads=[Wsb, YSB], writes=[b4])
                    op("dve", lambda: V.tensor_tensor(out=M2[:], in0=GB[:], in1=b4[:, :], op=ALU.mult), reads=[GB, b4], writes=[M2])
                    op("dve", lambda: V.tensor_tensor(out=mT[:, of, :], in0=M1[:], in1=M2[:], op=ALU.add), reads=[M1, M2], writes=[mT])
                    if pend:
                        pend.pop(0)()
                while pend:
                    pend.pop(0)()
                for tb in range(4):
                    if tb >= 2:
                        Bt(tc, tb - 2)
                    Bm(tc, tb)
                    if tb >= 1:
                        Bn(tc, tb - 1)
                pend.append(lambda tc=tc: Bn(tc, 3))
                pend.append(lambda tc=tc: Bt(tc, 2))
                pend.append(lambda tc=tc: Bt(tc, 3))
            while pend:
                pend.pop(0)()
            fw.barrier()
        stDE.close()
        print("phaseE12 ninst", fw.ninst, "nwaits", fw.nwaits)

        with contextlib.ExitStack() as st:
            W1 = fw.sb(st, [128, 8, 4096], BF16, "W1")
            W2 = fw.sb(st, [128, 32, 1024], BF16, "W2")
            W1v = [fw.view(W1) for _ in range(8)]
            W2v = [fw.view(W2) for _ in range(4)]
            w1bv = w1_b.h.rearrange("(k p) c -> p k c", p=128)
            w2bv = w2_b.h.rearrange("(k p) c -> p k c", p=128)
            n2c = [fw.sb(st, [128, 8, 512], BF16, "n2c")] * 2
            n2Tv = n2T_d.h.rearrange("(k p) t -> p k t", p=128)
            h2Tv = h2T_d.h.rearrange("(k p) t -> p k t", p=128)
            dma("sp", n2c[0][:], n2Tv[:, :, 0:512], n2c[0], n2T_d, n2c[0])
            for j in range(8):
                dma("sp" if j % 2 == 0 else "act", W1[:, :, j * 512:(j + 1) * 512], w1bv[:, :, j * 512:(j + 1) * 512], W1v[j], w1_b, W1v[j])
            for j in range(4):
                dma("sp" if j % 2 == 0 else "act", W2[:, j * 8:(j + 1) * 8, :], w2bv[:, j * 8:(j + 1) * 8, :], W2v[j], w2_b, W2v[j])
            nfpost = rep_load(st, nfpost_d, 1024, "nfpost")
            aTt = fw.sb(st, [128, 32, 512], BF16, "aT")
            r_t = [fw.sb(st, [128, 512], F32, "r_t") for _ in range(2)] + [None]
            h1b = [fw.sb(st, [128, 1024], F32, "h1b") for _ in range(2)]
            h2t = [fw.sb(st, [128, 1024], F32, "h2t")] * 2
            hb16 = fw.sb(st, [128, 1024], BF16, "hb16")
            h2st = fw.sb(st, [128, 8, 128], BF16, "h2st")
            ss2 = [fw.sb(st, [128, 4], F32, "ss2b") for _ in range(2)]
            junkE = fw.sb(st, [128, 512], F32, "junkE2")
            pend = []

            def Fm(tc, tb):
                blk = 4 * tc + tb
                bks = [banks[(blk % 2) * 2], banks[(blk % 2) * 2 + 1]]
                dma("sp", h1b[blk % 2][:], h1_d[blk * 128:(blk + 1) * 128, :], h1b[blk % 2], h1_d, h1b[blk % 2])
                for half in range(2):
                    for kc in range(32):
                        op("pe", lambda: PE.matmul(bks[half][:, :], lhsT=aTt[:, kc, tb * 128:(tb + 1) * 128], rhs=W2[:, kc, half * 512:(half + 1) * 512],
                                                   start=(kc == 0), stop=(kc == 31)), reads=[aTt, W2v[kc // 8]], writes=[bks[half]])

            def Fn(tc, tb):
                blk = 4 * tc + tb
                H1B, H2, SS = h1b[blk % 2], h2t[blk % 2], ss2[blk % 2]
                bks = [banks[(blk % 2) * 2], banks[(blk % 2) * 2 + 1]]
                norm_residual(SS, junkE, bks, nfpost, H1B, H2)
                dma("sp", h2_d[blk * 128:(blk + 1) * 128, :], H2[:], h2_d, H2, H2)
                op("act", lambda: S.copy(out=hb16[:], in_=H2[:]), reads=[H2], writes=[hb16])

            def Ft(tc, tb):
                blk = 4 * tc + tb
                bk = banks[(blk % 2) * 2]
                transposes_to(hb16, bk, lambda pT: op("act", lambda: S.copy(out=h2st[:], in_=pT), reads=[bk], writes=[h2st]))
                dma("sp", h2Tv[:, :, blk * 128:(blk + 1) * 128], h2st[:], h2T_d, h2st, h2st)

            hfi = 0
            for tc in range(4):
                N2 = n2c[tc % 2]
                if tc >= 1:
                    dma("sp", N2[:], n2Tv[:, :, tc * 512:(tc + 1) * 512], N2, n2T_d, N2)
                for hf in range(32):
                    RT = r_t[hfi % 2]
                    bk = banks[4 + hfi % 4]
                    hfi += 1
                    for kc in range(8):
                        op("pe", lambda: PE.matmul(bk[:, :], lhsT=W1[:, kc, hf * 128:(hf + 1) * 128], rhs=N2[:, kc, :], start=(kc == 0), stop=(kc == 7)), reads=[W1v[hf // 4], N2], writes=[bk])
                    op("act", lambda: S.activation(out=RT[:], in_=bk[:, :], func=AF.Relu), reads=[bk], writes=[RT])
                    op("dve", lambda: V.tensor_tensor(out=aTt[:, hf, :], in0=RT[:], in1=RT[:], op=ALU.mult), reads=[RT], writes=[aTt])
                    if pend and hf % 4 == 3:
                        pend.pop(0)()
                while pend:
                    pend.pop(0)()
                for tb in range(4):
                    Fm(tc, tb)
                    if tb >= 1:
                        Fn(tc, tb - 1)
                        Ft(tc, tb - 1)
                pend.append(lambda tc=tc: Fn(tc, 3))
                pend.append(lambda tc=tc: Ft(tc, 3))
            while pend:
                pend.pop(0)()
            fw.barrier()
        print("phaseE34 ninst", fw.ninst, "nwaits", fw.nwaits)

        with contextlib.ExitStack() as st:
            Wpg = fw.sb(st, [128, 8, 1024], BF16, "Wpg")
            Wp = fw.sb(st, [128, 2, 1024], BF16, "Wp")
            dma("sp", Wpg[:], wpg_b.h.rearrange("(k p) c -> p k c", p=128), Wpg, wpg_b, Wpg)
            dma("act", Wp[:], wp_b.h.rearrange("(k p) c -> p k c", p=128), Wp, wp_b, Wp)
            nppost = rep_load(st, nppost_d, 1024, "nppost")
            h2b = [fw.sb(st, [128, 1024], F32, "h2b") for _ in range(3)]
            h2Tb = [fw.sb(st, [128, 8, 128], BF16, "h2Tb") for _ in range(3)]
            pbt = [fw.sb(st, [128, 256], F32, "pbt") for _ in range(3)]
            pb16 = [fw.sb(st, [128, 256], BF16, "pb16") for _ in range(2)]
            pTt = [fw.sb(st, [128, 2, 128], BF16, "pTt") for _ in range(2)]
            sg = [fw.sb(st, [128, 1024], F32, "sg") for _ in range(2)]
            prod = [fw.sb(st, [128, 1024], F32, "prod") for _ in range(2)]
            ot = [fw.sb(st, [128, 1024], F32, "ot") for _ in range(2)]
            ss1 = [fw.sb(st, [128, 1], F32, "ss1") for _ in range(2)]
            junkE = fw.sb(st, [128, 1024], F32, "junkE3")
            h2Tv = h2T_d.h.rearrange("(k p) t -> p k t", p=128)

            def loadP(blk):
                if blk >= NOB:
                    return
                dma("sp", h2b[blk % 3][:], h2_d[blk * 128:(blk + 1) * 128, :], h2b[blk % 3], h2_d, h2b[blk % 3])
                dma("sp", h2Tb[blk % 3][:], h2Tv[:, :, blk * 128:(blk + 1) * 128], h2Tb[blk % 3], h2T_d, h2Tb[blk % 3])
                dma("sp", pbt[blk % 3][:], p_d[blk * 128:(blk + 1) * 128, :], pbt[blk % 3], p_d, pbt[blk % 3])

            def Pm(blk):
                H2TB, PB, PB16, PT, SG, PR = h2Tb[blk % 3], pbt[blk % 3], pb16[blk % 2], pTt[blk % 2], sg[blk % 2], prod[blk % 2]
                op("pool", lambda: G.tensor_copy(out=PB16[:], in_=PB[:]), reads=[PB], writes=[PB16])
                bk = bank()
                pT = bk.hb.rearrange("p (k c) -> p k c", k=8)
                for k in range(2):
                    op("pe", lambda: PE.transpose(out=pT[:, k, :], in_=PB16[:, k * 128:(k + 1) * 128], identity=ident[:]), reads=[PB16, ident], writes=[bk])
                op("act", lambda: S.copy(out=PT[:], in_=pT[:, 0:2, :]), reads=[bk], writes=[PT])
                for half in range(2):
                    hs = slice(half * 512, (half + 1) * 512)
                    bg_, bp_ = bank(), bank()
                    for kc in range(8):
                        op("pe", lambda: PE.matmul(bg_[:, :], lhsT=H2TB[:, kc, :], rhs=Wpg[:, kc, hs], start=(kc == 0), stop=(kc == 7)), reads=[H2TB, Wpg], writes=[bg_])
                    op("act", lambda: S.activation(out=SG[:, hs], in_=bg_[:, :], func=AF.Sigmoid), reads=[bg_], writes=[SG])
                    for kc in range(2):
                        op("pe", lambda: PE.matmul(bp_[:, :], lhsT=PT[:, kc, :], rhs=Wp[:, kc, hs], start=(kc == 0), stop=(kc == 1)), reads=[PT, Wp], writes=[bp_])
                    op("dve", lambda: V.tensor_tensor(out=PR[:, hs], in0=SG[:, hs], in1=bp_[:, :], op=ALU.mult), reads=[SG, bp_], writes=[PR])

            def Pn(blk):
                H2B, PR, OT, SS = h2b[blk % 3], prod[blk % 2], ot[blk % 2], ss1[blk % 2]
                op("dve", lambda: V.scalar_tensor_tensor(out=junkE[:], in0=PR[:], scalar=1.0, in1=PR[:], op0=ALU.mult, op1=ALU.mult, accum_out=SS[:]), reads=[PR], writes=[SS])
                op("act", lambda: S.activation(out=SS[:], in_=SS[:], func=AF.Sqrt, scale=1.0 / 1024, bias=EPS), reads=[SS], writes=[SS])
                op("dve", lambda: V.reciprocal(out=SS[:], in_=SS[:]), reads=[SS], writes=[SS])
                op("dve", lambda: V.scalar_tensor_tensor(out=OT[:], in0=PR[:], scalar=SS[:], in1=nppost[:], op0=ALU.mult, op1=ALU.mult), reads=[PR, SS, nppost], writes=[OT])
                op("dve", lambda: V.tensor_tensor(out=OT[:], in0=OT[:], in1=H2B[:], op=ALU.add), reads=[OT, H2B], writes=[OT])
                dma("sp", out_d[blk * 128:(blk + 1) * 128, :], OT[:], out_d, OT, OT)

            loadP(0)
            loadP(1)
            for blk in range(NOB):
                if blk >= 1:
                    Pn(blk - 1)
                    loadP(blk + 1)
                Pm(blk)
            Pn(NOB - 1)
            fw.barrier(recycle=False)
        print("total ninst", fw.ninst, "nwaits", fw.nwaits, "nsems", 5 + len(fw.dma_T))
    return nc

    return nc


def host_inputs(inputs):
    x = np.asarray(inputs["x"], dtype=np.float32)
    p = np.asarray(inputs["p"], dtype=np.float32)[0]
    shared = {}
    for k in ("norm_mix_pre", "w_in", "conv_w", "conv_b", "dt_bias", "a_log", "d_skip", "ssd_norm", "w_ssd_branch",
              "w_sb_branch", "w_gate", "b_gate", "w_out", "norm_mix_post", "norm_ffn_pre", "w_ff1", "w_ff2",
              "norm_ffn_post", "w_ple", "w_ple_gate", "norm_ple_post"):
        shared[k] = np.ascontiguousarray(np.asarray(inputs[k], dtype=np.float32)[0])
    maps = []
    for c in range(8):
        b, r = c // 2, c % 2
        xs = np.zeros((NPOS, 1024), np.float32)
        if r == 0:
            xs[128:] = x[b, :NPOS - 128]
        else:
            xs[:] = x[b]
        own = p[b].reshape(NBLK, 128, 256)[r::2].reshape(NOWN, 256)
        valid = np.ones((128, NBLK), np.float32)
        if r == 0:
            valid[:, 0] = 0.0
        m = dict(shared)
        m["xs"] = xs
        m["p_own"] = np.ascontiguousarray(own)
        m["valid"] = valid
        maps.append(m)
    return maps


def kernel(**inputs):
    nc = build()
    maps = host_inputs(inputs)
    res = run_bass_kernel_spmd(nc, maps, core_ids=list(range(8)))
    out = np.zeros((4, NBLK, 128, 1024), np.float32)
    for c in range(8):
        b, r = c // 2, c % 2
        out[b, r::2] = np.asarray(res.results[c]["out"], dtype=np.float32).reshape(NOB, 128, 1024)
    return out.reshape(4, 4096, 1024)
```
